# Optimizing a Trainium2 kernel written in Bass

```python
import jax, jax.numpy as jnp
from jax import lax
import numpy as np

D_MODEL = 2048
BATCH = 16
SEQ = 2048
DEPTH = 1
DEC_BATCH = 1
DEC_SEQ = 8192
PAST_LEN = 128

ATTN_WIDTH = D_MODEL // 2
FOURIER_WIDTH = D_MODEL - ATTN_WIDTH
HEAD_DIM = 128
N_ATTN_HEADS = ATTN_WIDTH // HEAD_DIM
N_FOURIER_GROUPS = 4
FOURIER_GROUP_DIM = FOURIER_WIDTH // N_FOURIER_GROUPS
D_FF = 4 * D_MODEL
DILATED_BRANCHES = ((128, 1), (512, 4), (2048, 16))
N_MOD = 6
RMS_EPS = 1e-6
NEG_INF = -1e30

kernel_name = "hybrid_dilated_attn_fourier_encoder"


def _rms(x, g):
    xf = x.astype(jnp.float32)
    y = xf * lax.rsqrt(jnp.mean(xf * xf, axis=-1, keepdims=True) + RMS_EPS)
    return (y * g.astype(jnp.float32)).astype(x.dtype)


def _alibi_slopes(n_heads):
    return 2.0 ** (-8.0 * (jnp.arange(n_heads, dtype=jnp.float32) + 1.0) / n_heads)


def _band_attention(q, k, v, slopes, half):
    G, L, H, Dh = q.shape
    w = half
    nb = -(-L // w)
    Lp = nb * w
    pad = Lp - L
    qb = jnp.pad(q, ((0, 0), (0, pad), (0, 0), (0, 0))).reshape(G, nb, w, H, Dh)
    kp = jnp.pad(k, ((0, 0), (w, pad + w), (0, 0), (0, 0))).reshape(G, nb + 2, w, H, Dh)
    vp = jnp.pad(v, ((0, 0), (w, pad + w), (0, 0), (0, 0))).reshape(G, nb + 2, w, H, Dh)
    kwin = jnp.concatenate([kp[:, :-2], kp[:, 1:-1], kp[:, 2:]], axis=2)
    vwin = jnp.concatenate([vp[:, :-2], vp[:, 1:-1], vp[:, 2:]], axis=2)
    s = jnp.einsum('gnqhd,gnkhd->gnhqk', qb, kwin).astype(jnp.float32)
    qpos = jnp.arange(Lp).reshape(nb, w)
    kpos = jnp.arange(nb)[:, None] * w - w + jnp.arange(3 * w)[None, :]
    absdist = jnp.abs(qpos[:, :, None] - kpos[:, None, :])
    valid = (absdist <= w) & (kpos >= 0)[:, None, :] & (kpos < L)[:, None, :]
    s = s - slopes[None, None, :, None, None] * absdist[None, :, None].astype(jnp.float32)
    s = jnp.where(valid[None, :, None], s, NEG_INF)
    m = jnp.max(s, axis=-1, keepdims=True)
    p = jnp.exp(s - m)
    den = jnp.sum(p, axis=-1, keepdims=True)
    o = jnp.einsum('gnhqk,gnkhd->gnqhd', p, vwin.astype(jnp.float32))
    o = o / jnp.transpose(den, (0, 1, 3, 2, 4))
    lse = jnp.transpose((m + jnp.log(den))[..., 0], (0, 1, 3, 2))
    o = o.reshape(G, Lp, H, Dh)[:, :L]
    lse = lse.reshape(G, Lp, H)[:, :L]
    return o, lse


def _dilated_branch(q, k, v, slopes, window, dil):
    B, S, H, Dh = q.shape
    L = S // dil

    def split(t):
        return t.reshape(B, L, dil, H, Dh).transpose(0, 2, 1, 3, 4).reshape(B * dil, L, H, Dh)

    o, lse = _band_attention(split(q), split(k), split(v), slopes * dil, window // (2 * dil))
    o = o.reshape(B, dil, L, H, Dh).transpose(0, 2, 1, 3, 4).reshape(B, S, H, Dh)
    lse = lse.reshape(B, dil, L, H).transpose(0, 2, 1, 3).reshape(B, S, H)
    return o, lse


def _dilated_attention(q, k, v):
    slopes = _alibi_slopes(q.shape[2])
    outs, lses = [], []
    for window, dil in DILATED_BRANCHES:
        o, l = _dilated_branch(q, k, v, slopes, window, dil)
        outs.append(o)
        lses.append(l)
    wts = jax.nn.softmax(jnp.stack(lses, axis=0), axis=0)
    out = jnp.sum(wts[..., None] * jnp.stack(outs, axis=0), axis=0)
    return out


def _fourier_mix(u, w_four):
    B, S, _ = u.shape
    ug = u.astype(jnp.float32).reshape(B, S, N_FOURIER_GROUPS, FOURIER_GROUP_DIM)
    f = jnp.fft.fft2(ug, axes=(1, 3), norm='ortho').real
    y = jnp.einsum('bsgc,gce->bsge', f, w_four.astype(jnp.float32))
    return y.reshape(B, S, FOURIER_WIDTH).astype(u.dtype)


def _layer(x, c, w_ada, b_ada, g_pre_mix, w_in, g_attn_out, w_fourier, g_fourier_out,
           w_out, g_post_mix, g_pre_mlp, w_mlp_in, w_mlp_out, g_post_mlp):
    B, S, _ = x.shape
    mod = jnp.matmul(jax.nn.silu(c), w_ada) + b_ada
    sh1, sc1, gt1, sh2, sc2, gt2 = jnp.split(mod[:, None, :], N_MOD, axis=-1)

    h = _rms(x, g_pre_mix) * (1.0 + sc1) + sh1
    z = jnp.matmul(h, w_in)
    q = z[..., :ATTN_WIDTH].reshape(B, S, N_ATTN_HEADS, HEAD_DIM) * (HEAD_DIM ** -0.5)
    k = z[..., ATTN_WIDTH:2 * ATTN_WIDTH].reshape(B, S, N_ATTN_HEADS, HEAD_DIM)
    v = z[..., 2 * ATTN_WIDTH:3 * ATTN_WIDTH].reshape(B, S, N_ATTN_HEADS, HEAD_DIM)
    u = z[..., 3 * ATTN_WIDTH:]
    a = _dilated_attention(q, k, v).reshape(B, S, ATTN_WIDTH).astype(x.dtype)
    f = _fourier_mix(u, w_fourier)
    mixed = jnp.concatenate([_rms(a, g_attn_out), _rms(f, g_fourier_out)], axis=-1)
    y = jnp.matmul(mixed, w_out)
    x = x + gt1 * _rms(y, g_post_mix)

    h = _rms(x, g_pre_mlp) * (1.0 + sc2) + sh2
    hid = jnp.square(jax.nn.relu(jnp.matmul(h, w_mlp_in)))
    y = jnp.matmul(hid, w_mlp_out)
    x = x + gt2 * _rms(y, g_post_mlp)
    return x


def _trunk(x, c, w_ada, b_ada, g_pre_mix, w_in, g_attn_out, w_fourier, g_fourier_out,
           w_out, g_post_mix, g_pre_mlp, w_mlp_in, w_mlp_out, g_post_mlp):
    for l in range(DEPTH):
        x = _layer(x, c, w_ada[l], b_ada[l], g_pre_mix[l], w_in[l], g_attn_out[l],
                   w_fourier[l], g_fourier_out[l], w_out[l], g_post_mix[l], g_pre_mlp[l],
                   w_mlp_in[l], w_mlp_out[l], g_post_mlp[l])
    return x


def setup_inputs(seed: int = 0) -> dict:
    key = jax.random.key(seed)
    ks = jax.random.split(key, 20)
    f32 = jnp.float32

    def nrm(k, shape, scale):
        return jax.random.normal(k, shape, f32) * scale

    def gain(k, n):
        return 1.0 + 0.05 * jax.random.normal(k, (DEPTH, n), f32)

    in_cols = 3 * ATTN_WIDTH + FOURIER_WIDTH
    return {
        "x_prompt": nrm(ks[0], (BATCH, SEQ, D_MODEL), 1.0),
        "x_sample": nrm(ks[1], (DEC_BATCH, DEC_SEQ, D_MODEL), 1.0),
        "c_prompt": nrm(ks[2], (BATCH, D_MODEL), 1.0),
        "c_sample": nrm(ks[3], (DEC_BATCH, D_MODEL), 1.0),
        "w_ada": nrm(ks[4], (DEPTH, D_MODEL, N_MOD * D_MODEL), 0.5 * D_MODEL ** -0.5),
        "b_ada": nrm(ks[5], (DEPTH, N_MOD * D_MODEL), 0.01),
        "g_pre_mix": gain(ks[6], D_MODEL),
        "w_in": nrm(ks[7], (DEPTH, D_MODEL, in_cols), D_MODEL ** -0.5),
        "g_attn_out": gain(ks[8], ATTN_WIDTH),
        "w_fourier": nrm(ks[9], (DEPTH, N_FOURIER_GROUPS, FOURIER_GROUP_DIM, FOURIER_GROUP_DIM), FOURIER_GROUP_DIM ** -0.5),
        "g_fourier_out": gain(ks[10], FOURIER_WIDTH),
        "w_out": nrm(ks[11], (DEPTH, ATTN_WIDTH + FOURIER_WIDTH, D_MODEL), (ATTN_WIDTH + FOURIER_WIDTH) ** -0.5),
        "g_post_mix": gain(ks[12], D_MODEL),
        "g_pre_mlp": gain(ks[13], D_MODEL),
        "w_mlp_in": nrm(ks[14], (DEPTH, D_MODEL, D_FF), D_MODEL ** -0.5),
        "w_mlp_out": nrm(ks[15], (DEPTH, D_FF, D_MODEL), D_FF ** -0.5),
        "g_post_mlp": gain(ks[16], D_MODEL),
    }


def reference(x_prompt, x_sample, c_prompt, c_sample, w_ada, b_ada, g_pre_mix, w_in,
              g_attn_out, w_fourier, g_fourier_out, w_out, g_post_mix, g_pre_mlp,
              w_mlp_in, w_mlp_out, g_post_mlp):
    y_prompt = _trunk(x_prompt, c_prompt, w_ada, b_ada, g_pre_mix, w_in, g_attn_out,
                      w_fourier, g_fourier_out, w_out, g_post_mix, g_pre_mlp,
                      w_mlp_in, w_mlp_out, g_post_mlp)
    y_sample = _trunk(x_sample, c_sample, w_ada, b_ada, g_pre_mix, w_in, g_attn_out,
                      w_fourier, g_fourier_out, w_out, g_post_mix, g_pre_mlp,
                      w_mlp_in, w_mlp_out, g_post_mlp)
    return (y_prompt, y_sample)
```

```python
import numpy as np
import ml_dtypes
from contextlib import ExitStack
import concourse.bass as bass
import concourse.mybir as mybir
from concourse.bass_utils import run_bass_kernel_spmd

F32 = mybir.dt.float32
BF16 = mybir.dt.bfloat16
AF = mybir.ActivationFunctionType
ALU = mybir.AluOpType
AX = mybir.AxisListType

NEG = -30000.0
EPS = 1e-6
SC = float(128 ** -0.5)
DBIG = 1.0e9
ENGS = ('pe', 'act', 'dve', 'pool', 'sp')
NDS = 8


class Op:
    __slots__ = ('eng', 'fn', 'deps', 'dma', 'ms', 'dj', 'sig')

    def __init__(self, eng, fn, deps, dma):
        self.eng = eng
        self.fn = fn
        self.deps = deps
        self.dma = dma
        self.ms = 0
        self.dj = -1
        self.sig = False


class Prog:
    def __init__(self):
        self.ops = []
        self.lw = {}
        self.rd = {}
        self.bar = set()
        self.bar_pending = set()
        self.last_on = {}
        self.all_dma = []

    def add(self, eng, fn, r=(), w=(), dma=False, nobar=False):
        deps = set()
        for k in r:
            i = self.lw.get(k)
            if i is not None:
                deps.add(i)
        for k in w:
            i = self.lw.get(k)
            if i is not None:
                deps.add(i)
            deps.update(self.rd.get(k, ()))
        if eng in self.bar_pending:
            deps.update(self.bar)
            self.bar_pending.discard(eng)
        idx = len(self.ops)
        self.ops.append(Op(eng, fn, deps, dma))
        for k in r:
            self.rd.setdefault(k, []).append(idx)
        for k in w:
            self.lw[k] = idx
            self.rd[k] = []
        if not nobar:
            self.last_on[eng] = idx
        if dma and not nobar:
            self.all_dma.append(idx)
        return idx

    def barrier(self):
        self.bar = set(self.last_on.values()) | set(self.all_dma)
        self.bar_pending = set(ENGS)

    def emit(self, nc, stack):
        ops = self.ops
        for op in ops:
            for d in op.deps:
                ops[d].sig = True
        cnt = {e: 0 for e in ENGS}
        dcnt = {e: 0 for e in ENGS}
        dlist = {e: [] for e in ENGS}
        for i, op in enumerate(ops):
            if op.dma:
                op.dj = dcnt[op.eng]
                dcnt[op.eng] += 1
                dlist[op.eng].append(i)
            elif op.sig:
                cnt[op.eng] += 1
                op.ms = cnt[op.eng]
        for e in ENGS:
            assert cnt[e] < 60000, (e, cnt[e])
        sem = {e: stack.enter_context(nc.semaphore("s_" + e)) for e in ENGS}
        dsem = {e: [stack.enter_context(nc.semaphore("d_%s%d" % (e, i))) for i in range(NDS)]
                for e in ENGS if dcnt[e] > 0}

        def completion(i):
            op = ops[i]
            if op.dma:
                return dsem[op.eng][op.dj % NDS], 16 * (op.dj // NDS + 1)
            return sem[op.eng], op.ms

        per = {e: [i for i, op in enumerate(ops) if op.eng == e] for e in ENGS}
        final = {}
        for e in ENGS:
            if cnt[e] > 0:
                final[sem[e]] = cnt[e]
            for j in range(dcnt[e]):
                s = dsem[e][j % NDS]
                final[s] = max(final.get(s, 0), 16 * (j // NDS + 1))

        def run(e, eng):
            seen = {}
            for i in per[e]:
                op = ops[i]
                waits = {}
                for d in op.deps:
                    if e == 'pe' and ops[d].eng == 'pe':
                        continue
                    s, v = completion(d)
                    if waits.get(s, 0) < v:
                        waits[s] = v
                if op.dma and op.dj >= NDS:
                    s, v = completion(dlist[e][op.dj - NDS])
                    if waits.get(s, 0) < v:
                        waits[s] = v
                for s, v in waits.items():
                    if seen.get(s, 0) < v:
                        eng.wait_ge(s, v)
                        seen[s] = v
                ins = op.fn(eng)
                if op.dma:
                    ins.then_inc(dsem[e][op.dj % NDS], 16)
                elif op.sig:
                    ins.then_inc(sem[e], 1)
            if e == 'sp':
                for s, v in final.items():
                    if seen.get(s, 0) < v:
                        eng.wait_ge(s, v)

        with nc.Block() as block:
            @block.tensor
            def _(eng):
                run('pe', eng)

            @block.scalar
            def _(eng):
                run('act', eng)

            @block.vector
            def _(eng):
                run('dve', eng)

            @block.gpsimd
            def _(eng):
                run('pool', eng)

            @block.sync
            def _(eng):
                run('sp', eng)


class Arena:
    def __init__(self, t, n):
        self.t = t
        self.n = n
        self.off = 0

    def reset(self):
        self.off = 0

    def f32(self, *shape):
        n = int(np.prod(shape))
        a = self.t[:, self.off:self.off + n]
        self.off += n
        assert self.off <= self.n, ("arena overflow", self.off, self.n)
        if len(shape) == 2:
            a = a.rearrange("p (a b) -> p a b", b=shape[1])
        elif len(shape) == 3:
            a = a.rearrange("p (a b c) -> p a b c", b=shape[1], c=shape[2])
        return a

    def bf16(self, *shape):
        n = int(np.prod(shape))
        assert n % 2 == 0
        a = self.t[:, self.off:self.off + n // 2].bitcast(BF16)
        self.off += n // 2
        assert self.off <= self.n, ("arena overflow", self.off, self.n)
        if len(shape) == 2:
            a = a.rearrange("p (a b) -> p a b", b=shape[1])
        elif len(shape) == 3:
            a = a.rearrange("p (a b c) -> p a b c", b=shape[1], c=shape[2])
        elif len(shape) == 4:
            a = a.rearrange("p (a b c d) -> p a b c d", b=shape[1], c=shape[2], d=shape[3])
        return a


def build(cfg=None):
    cfg = cfg or {}
    dump = cfg.get('dump', ())
    stop_after = cfg.get('stop_after', 99)
    WINS = cfg.get('wins', (0, 1, 2))
    nc = bass.Bass("TRN2", target_bir_lowering=False)
    P = Prog()
    stack = ExitStack()

    def din(n, s, dt=F32):
        return nc.dram_tensor(n, list(s), dt, kind="ExternalInput").ap()

    def dscr(n, s, dt=BF16):
        kind = "ExternalOutput" if n in dump else "Internal"
        return nc.dram_tensor(n, list(s), dt, kind=kind).ap()

    xp = din("xp", [2, 2048, 2048])
    xsw = din("xsw", [3072, 2048])
    xsf = din("xsf", [8192, 2048])
    ccol_d = din("ccol", [128, 48])
    w_ada = din("w_ada", [2048, 12288])
    b_ada_col_d = din("b_ada_col", [128, 96])
    b_ada_row = din("b_ada_row", [1, 12288])
    gcols_d = din("gcols", [128, 48])
    grows = din("grows", [2, 2048])
    w_in = din("w_in", [2048, 4096])
    w_four = din("w_four", [4, 256, 256])
    w_out = din("w_out", [2048, 2048])
    w_mi = din("w_mi", [2048, 8192])
    w_mo = din("w_mo", [8192, 2048])
    tabc_d = din("tabc", [128, 1024], BF16)
    tabp = din("tabp", [4, 128, 2 * 16 * 512], BF16)
    tabs = din("tabs", [2, 8, 128, 2 * 8 * 512], BF16)
    dmask_d = din("dmask", [128, 256])
    edge_d = din("edge", [128, 8])
    identf_d = din("identf", [128, 128])
    identb_d = din("identb", [128, 128], BF16)

    yp = nc.dram_tensor("yp", [2, 2048, 2048], F32, kind="ExternalOutput").ap()
    ys = nc.dram_tensor("ys", [1024, 2048], F32, kind="ExternalOutput").ap()

    Wb_in_n = dscr("Wb_in", [2048, 4096])
    Wb_in = Wb_in_n.rearrange("(dc p) (pc n) -> pc p dc n", p=128, n=512)
    Wb_out = dscr("Wb_out", [4, 128, 16, 512])
    Wb_mi_n = dscr("Wb_mi", [2048, 8192])
    Wb_mi = Wb_mi_n.rearrange("(dc p) (fg n) -> fg p dc n", p=128, n=512)
    Wb_mo_n = dscr("Wb_mo", [8192, 2048])
    Wb_mo = Wb_mo_n.rearrange("(pcs f p) (nb n) -> nb pcs p f n", p=128, f=16, n=512)
    GP = dscr("GP", [3, 2, 2048], F32)
    SOWN = [2048, 2048, 1024]
    KLEN = [2048, 2048, 3072]
    QT = [dscr("QT%d" % w, [8, 128, SOWN[w]]) for w in range(3)]
    KT = [dscr("KT%d" % w, [8, 128, KLEN[w]]) for w in range(3)]
    VV = [dscr("VV%d" % w, [4096, 1024]) for w in range(3)]
    PQ = [dscr("PQ%d" % w, [[2048, 2048, 8192][w], 2048]) for w in range(3)]
    AT = [dscr("AT%d" % w, [1024, SOWN[w]]) for w in range(3)]
    FT = [dscr("FT%d" % w, [1024, SOWN[w]]) for w in range(3)]

    def sb(name, shape, dt):
        return stack.enter_context(nc.sbuf_tensor(name, list(shape), dt))

    identb = sb("identb_s", [128, 128], BF16)
    identf = sb("identf_s", [128, 128], F32)
    onesb = sb("onesb", [128, 128], BF16)
    dm = sb("dm", [128, 256], F32)
    edge = sb("edge_s", [128, 8], F32)
    cols_t = sb("cols", [128, 4 * 16 * 3], F32)
    cols = cols_t[:, :].rearrange("p (a b c) -> p a b c", a=4, b=16)
    gcols_t = sb("gcols_s", [128, 48], F32)
    gcols = gcols_t[:, :].rearrange("p (a b) -> p a b", b=16)
    AB_t = sb("AB", [128, 4 * 2 * 512], BF16)
    AB = AB_t[:, :].rearrange("p (g c n) -> p g c n", g=4, c=2)
    modcol_t = sb("modcol", [128, 288], F32)
    modcol = modcol_t[:, :].rearrange("p (a b c) -> p a b c", a=6, b=16)
    bcol_t = sb("bcol", [128, 96], F32)
    bcol = bcol_t[:, :].rearrange("p (a b) -> p a b", b=16)
    scT_t = sb("scT", [128, 48], BF16)
    scT = scT_t[:, :].rearrange("p (a b) -> p a b", b=3)
    ssqa = sb("ssqa", [128, 40], F32)
    epsc = sb("epsc", [128, 1], F32)
    ssqf = sb("ssqf", [128, 40], F32)
    rsa = sb("rsa", [128, 40], F32)
    rsf = sb("rsf", [128, 40], F32)
    NA = 48800
    arena_t = sb("arena", [128, NA], F32)
    ar = Arena(arena_t, NA)
    ps = [stack.enter_context(nc.psum_tensor("ps%d" % i, [128, 512], F32)) for i in range(8)]
    psb = [p[:, 0:256].bitcast(BF16) for p in ps]

    evac_ctr = [0]

    def evac_copy(out, in_, r, w):
        evac_ctr[0] += 1
        if evac_ctr[0] % 2 == 0:
            P.add('act', lambda e: e.activation(out=out, in_=in_, func=AF.Copy), r=r, w=w)
        else:
            P.add('dve', lambda e: e.tensor_copy(out=out, in_=in_), r=r, w=w)

    def mm_group(out, pairs, r, w):
        def fn(pe):
            n = len(pairs)
            ins = None
            for i, (l, rr) in enumerate(pairs):
                ins = pe.matmul(out, l, rr, start=(i == 0), stop=(i == n - 1))
            return ins
        P.add('pe', fn, r=r, w=w)

    def load(out, in_, w, r=(), eng='sp', nobar=False):
        P.add(eng, lambda e: e.dma_start(out=out, in_=in_), r=r, w=w, dma=True, nobar=nobar)

    def store(out, in_, r, w=(), eng='act'):
        P.add(eng, lambda e: e.dma_start(out=out, in_=in_), r=r, w=w, dma=True)

    load(identb[:, :], identb_d[:, :], w=['identb'])
    load(identf[:, :], identf_d[:, :], w=['identf'])
    load(dm[:, :], dmask_d[:, :], w=['dm'])
    load(edge[:, :], edge_d[:, :], w=['edge'])
    load(gcols_t[:, :], gcols_d[:, :], w=['gcols'])
    P.add('dve', lambda e: e.memset(onesb[:, :], 1.0), w=['onesb'])
    P.add('dve', lambda e: e.memset(epsc[:, :], EPS), w=['epsc'])
    P.add('dve', lambda e: e.memset(ssqa[:, :], 1.0), w=['ssqa'])
    P.add('dve', lambda e: e.memset(ssqf[:, :], 1.0), w=['ssqf'])

    for pc in range(8):
        load(Wb_in_n[pc * 256:(pc + 1) * 256, :], w_in[pc * 256:(pc + 1) * 256, :], w=[('Wb_in', pc)], eng='pool')

    ar.reset()
    zt = arena_t[:, NA - 1024:NA].bitcast(BF16).rearrange("p (a b) -> p a b", b=1024)
    P.add('dve', lambda e: e.memset(zt, 0.0), w=['zt'])

    cT = ar.f32(48)
    wa = [ar.bf16(16, 512) for _ in range(2)]
    wa32 = [ar.f32(16, 512) for _ in range(2)]
    load(cT, ccol_d[:, :], w=['cT'])
    P.add('dve', lambda e: e.memset(modcol, 0.0), w=['modcol'])
    load(bcol, b_ada_col_d[:, :].rearrange("p (a b) -> p a b", b=16), w=['bcol'])
    P.add('act', lambda e: e.activation(out=scT, in_=cT.rearrange("p (a b) -> p a b", b=3), func=AF.Silu),
          r=['cT'], w=['scT'])
    w_ada_v = w_ada.rearrange("(dc p) (j n) -> j p dc n", p=128, n=512)
    gstate = [0]

    def mod_block(j, wj, kw, pcol, pgate, gbufs):
        typ, jj = j // 4, j % 4
        if typ in (2, 5):
            gi = gstate[0]
            gstate[0] += 1
            gate = 0 if typ == 2 else 1
            pk = ('ps', pgate)
            pst = ps[pgate]
            mm_group(pst[0:3, :], [(scT[:, dc, :], wj[:, dc, :]) for dc in range(16)],
                     r=['scT', kw], w=[pk])
            br, gr, gt = gbufs[0][gi % 2], gbufs[1][gi % 2], gbufs[2][gi % 2]
            c0 = typ * 2048 + jj * 512
            load(br[0:3, :], b_ada_row[0:1, c0:c0 + 512].partition_broadcast(3)[:, 0, :], w=[('brow', gi % 2)])
            load(gr[0:3, :], grows[gate:gate + 1, jj * 512:(jj + 1) * 512].partition_broadcast(3)[:, 0, :],
                 w=[('grow', gi % 2)])
            P.add('dve', lambda e: e.tensor_tensor(out=gt[0:3, :], in0=pst[0:3, :], in1=br[0:3, :], op=ALU.add),
                  r=[pk, ('brow', gi % 2)], w=[('gtmp', gi % 2)])
            P.add('dve', lambda e: e.tensor_tensor(out=gt[0:3, :], in0=gt[0:3, :], in1=gr[0:3, :], op=ALU.mult),
                  r=[('grow', gi % 2)], w=[('gtmp', gi % 2), ('brow', gi % 2)])
            store(GP[:, gate, jj * 512:(jj + 1) * 512], gt[0:3, :], r=[('gtmp', gi % 2)], w=[('gtmp', gi % 2)])
        else:
            pk = ('ps', pcol)
            pst = ps[pcol]

            def fn(pe):
                ins = None
                for t in range(4):
                    for dc in range(16):
                        ins = pe.matmul(pst[:, t * 3:t * 3 + 3], wj[:, dc, t * 128:(t + 1) * 128], scT[:, dc, :],
                                        start=(dc == 0), stop=(dc == 15))
                return ins
            P.add('pe', fn, r=['scT', kw], w=[pk])
            P.add('dve', lambda e: e.tensor_copy(
                out=modcol[:, typ, jj * 4:(jj + 1) * 4, :], in_=pst[:, 0:12].rearrange("p (a b) -> p a b", b=3)),
                r=[pk], w=['modcol'])

    def mod_finish(t_sh, t_sc, k_g, k_sh, gsel, key):
        for typ in (t_sh, t_sc):
            P.add('dve', lambda e, typ=typ: e.tensor_tensor(out=modcol[:, typ], in0=modcol[:, typ],
                                                            in1=bcol[:, typ].unsqueeze(2).to_broadcast([128, 16, 3]), op=ALU.add),
                  r=['bcol'], w=['modcol'])
        P.add('dve', lambda e: e.scalar_tensor_tensor(
            out=cols[:, k_g], in0=modcol[:, t_sc], scalar=1.0,
            in1=gcols[:, gsel, :].unsqueeze(2).to_broadcast([128, 16, 3]), op0=ALU.add, op1=ALU.mult),
            r=['modcol', 'gcols'], w=[key])
        P.add('dve', lambda e: e.tensor_copy(out=cols[:, k_sh], in_=modcol[:, t_sh]), r=['modcol'], w=[key])

    for j in range(8):
        wj = wa[j % 2]
        kw = ('wa', j % 2)
        w32 = wa32[j % 2]
        k32 = ('wa32', j % 2)
        load(w32, w_ada_v[j], w=[k32])
        P.add('dve', lambda e, wj=wj, w32=w32: e.tensor_copy(out=wj[:, 0:8, :], in_=w32[:, 0:8, :]), r=[k32], w=[kw])
        P.add('act', lambda e, wj=wj, w32=w32: e.activation(out=wj[:, 8:16, :], in_=w32[:, 8:16, :], func=AF.Copy), r=[k32], w=[kw])
        mod_block(j, wj, kw, j % 2, 4, None)
    mod_finish(0, 1, 0, 1, 0, 'cols1')

    wf = ar.bf16(4, 2, 256)
    tcs = ar.bf16(2, 2, 256)
    load(wf, w_four.rearrange("g (cc p) e -> p g cc e", p=128), w=['wf'], eng='pool')
    load(tcs, tabc_d[:, :].rearrange("p (a b c) -> p a b c", a=2, b=2), w=['tcs'])
    for g in range(4):
        for cc in range(2):
            pk = ('ps', 6 + (g * 2 + cc) % 2)
            pst = ps[6 + (g * 2 + cc) % 2]

            def fn(pe, pst=pst, g=g, cc=cc):
                ins = None
                for s_ in range(2):
                    for c2 in range(2):
                        ins = pe.matmul(pst[:, s_ * 256:(s_ + 1) * 256], tcs[:, s_, c2, cc * 128:(cc + 1) * 128],
                                        wf[:, g, c2, :], start=(c2 == 0), stop=(c2 == 1))
                return ins
            P.add('pe', fn, r=['wf', 'tcs'], w=[pk])
            evac_copy(AB[:, g, cc, :], pst[:, :], r=[pk], w=['AB'])

    for fg in range(16):
        load(Wb_mi_n[fg * 128:(fg + 1) * 128, :], w_mi[fg * 128:(fg + 1) * 128, :], w=[('Wb_mi', fg)], eng='pool', nobar=True)
    for nt in range(16):
        load(Wb_mo_n[nt * 512:(nt + 1) * 512, :], w_mo[nt * 512:(nt + 1) * 512, :], w=[('Wb_mo', nt)], eng='pool', nobar=True)
    for w in WINS:
        regs = [(0, 1024), (3072, 4096)] if w < 2 else [(3072, 4096)]
        for (a, b) in regs:
            for r0 in range(a, b, 256):
                P.add('pool', lambda e, w=w, r0=r0: e.dma_start(out=VV[w][r0:r0 + 256, :].rearrange("(a p) n -> p a n", p=128), in_=zt),
                      r=['zt'], w=[('VVz', w, r0)], dma=True, nobar=True)
    P.barrier()

    if stop_after >= 1:
        ar.reset()
        xt = [ar.f32(2048) for _ in range(4)]
        xs = [ar.bf16(2048) for _ in range(4)]
        junk = ar.bf16(2048)
        hTs = [[ar.bf16(512) for _ in range(16)] for _ in range(2)]
        wb = [ar.bf16(16, 512) for _ in range(2)]
        uT = [ar.bf16(512) for _ in range(8)]
        stq = [ar.bf16(512) for _ in range(4)]
        stv = ar.bf16(4, 1024)
        stpq = ar.bf16(4, 2048)
        st1 = ar.f32(8)
        blocks = []
        for w in WINS:
            if w < 2:
                for k in range(4):
                    blocks.append(dict(x=[xp[w, k * 512 + t * 128:k * 512 + (t + 1) * 128, :] for t in range(4)],
                                       b=w, w=w, q=k * 512, k=k * 512, v=1024 + k * 512, u=k * 512, pqw=w))
            else:
                for k in range(6):
                    blocks.append(dict(x=[xsw[k * 512 + t * 128:k * 512 + (t + 1) * 128, :] for t in range(4)],
                                       b=2, w=2, q=(k * 512 - 1024 if k in (2, 3) else None), k=k * 512, v=k * 512,
                                       u=None, pqw=2))
                for k in range(16):
                    blocks.append(dict(x=[xsf[k * 512 + t * 128:k * 512 + (t + 1) * 128, :] for t in range(4)],
                                       b=2, w=2, q=None, k=None, v=None, u=k * 512, pqw=2))
        wslot = [0]
        psr = [0]
        sq_i = [0]

        def nextps():
            i = 2 + psr[0] % 6
            psr[0] += 1
            return i

        def prep(bi_):
            blk = blocks[bi_]
            b = blk['b']
            hs = bi_ % 2
            hT = hTs[hs]
            for t in range(4):
                load(xt[t], blk['x'][t], w=[('xt', t)])
                P.add('act', lambda e, t=t: e.activation(out=junk, in_=xt[t], func=AF.Square, accum_out=st1[:, t:t + 1]),
                      r=[('xt', t)], w=['junk', ('st1', t)])
                P.add('act', lambda e, t=t: e.activation(out=st1[:, 4 + t:5 + t], in_=st1[:, t:t + 1], func=AF.Sqrt, scale=1.0 / 2048, bias=epsc[:, 0:1]),
                      r=[('st1', t)], w=[('st1b', t)])
                P.add('dve', lambda e, t=t: e.reciprocal(out=st1[:, 4 + t:5 + t], in_=st1[:, 4 + t:5 + t]),
                      r=[('st1b', t)], w=[('st1b', t)])
                P.add('dve', lambda e, t=t: e.tensor_scalar(out=xs[t], in0=xt[t], scalar1=st1[:, 4 + t:5 + t], scalar2=None,
                                                            op0=ALU.mult),
                      r=[('xt', t), ('st1b', t)], w=[('xs', t)])
            yield
            for dc in range(16):
                pi = dc % 2

                def fn(pe, dc=dc, pi=pi):
                    ins = None
                    for t in range(4):
                        ins = pe.transpose(psb[pi][:, t * 128:(t + 1) * 128], xs[t][:, dc * 128:(dc + 1) * 128], identb[:, :])
                    return ins
                P.add('pe', fn, r=[('xs', t) for t in range(4)] + ['identb'], w=[('ps', pi)])
                P.add('act', lambda e, dc=dc, pi=pi, b=b, hT=hT: e.activation(out=hT[dc], in_=psb[pi][:, 0:512], func=AF.Identity,
                                                                           bias=cols[:, 1, dc, b:b + 1], scale=cols[:, 0, dc, b:b + 1]),
                      r=[('ps', pi), 'cols1'], w=[('hT', hs, dc)])
                yield

        pgen = [None]

        def pstep():
            if pgen[0] is not None:
                try:
                    next(pgen[0])
                except StopIteration:
                    pgen[0] = None

        def pdrain():
            while pgen[0] is not None:
                pstep()

        pgen[0] = prep(0)
        pdrain()
        for bi_, blk in enumerate(blocks):
            b = blk['b']
            w = blk['w']
            hT = hTs[bi_ % 2]
            hkeys = [('hT', bi_ % 2, dc) for dc in range(16)]
            need = []
            if blk['q'] is not None:
                need += [0, 1]
            if blk['k'] is not None:
                need += [2, 3]
            if blk['v'] is not None:
                need += [4, 5]
            if blk['u'] is not None:
                need += [6, 7]
            pdrain()
            if bi_ + 1 < len(blocks):
                pgen[0] = prep(bi_ + 1)
            for pci, pc in enumerate(need):
                sl = wslot[0] % 2
                wslot[0] += 1
                wk = ('wb', sl)
                load(wb[sl], Wb_in[pc], w=[wk])
                if pc < 4 or pc >= 6:
                    for t in range(4):
                        pi = nextps()
                        mm_group(ps[pi][:, :], [(wb[sl][:, dc, t * 128:(t + 1) * 128], hT[dc]) for dc in range(16)],
                                 r=hkeys + [wk], w=[('ps', pi)])
                        pstep()
                        if pc < 4:
                            h = (pc % 2) * 4 + t
                            si = sq_i[0] % 4
                            sq_i[0] += 1
                            evac_copy(stq[si], ps[pi][:, :], r=[('ps', pi)], w=[('stq', si)])
                            if pc < 2:
                                dst = QT[w][h, :, blk['q']:blk['q'] + 512]
                            else:
                                dst = KT[w][h, :, blk['k']:blk['k'] + 512]
                            store(dst, stq[si], r=[('stq', si)], w=[('stq', si)])
                        else:
                            ct = (pc - 6) * 4 + t
                            evac_copy(uT[ct], ps[pi][:, :], r=[('ps', pi)], w=[('uT', ct)])
                            if ct % 2 == 1:
                                g = ct // 2
                                for tt in range(4):
                                    pj = nextps()
                                    mm_group(ps[pj][:, :], [(uT[2 * g + cc][:, tt * 128:(tt + 1) * 128], AB[:, g, cc, :])
                                                            for cc in range(2)],
                                             r=[('uT', 2 * g), ('uT', 2 * g + 1), 'AB'], w=[('ps', pj)])
                                    pstep()
                                    o = stpq[:, tt, g * 512:(g + 1) * 512].rearrange("p (j q n) -> p q j n", j=2, q=2)
                                    i_ = ps[pj][:, :].rearrange("p (q j n) -> p q j n", q=2, j=2)
                                    evac_copy(o, i_, r=[('ps', pj)], w=[('stpq', tt)])
                    if pc == 7:
                        u0 = blk['u']
                        store(PQ[blk['pqw']][u0:u0 + 512, :].rearrange("(t p) n -> p t n", p=128), stpq,
                              r=[('stpq', tt) for tt in range(4)], w=[('stpq', tt) for tt in range(4)])
                else:
                    half = pc - 4
                    for tt in range(4):
                        pi = nextps()
                        mm_group(ps[pi][:, :], [(hT[dc][:, tt * 128:(tt + 1) * 128], wb[sl][:, dc, :]) for dc in range(16)],
                                 r=hkeys + [wk], w=[('ps', pi)])
                        pstep()
                        evac_copy(stv[:, tt, half * 512:(half + 1) * 512], ps[pi][:, :], r=[('ps', pi)], w=[('stv', tt)])
                    if pc == 5:
                        v0 = blk['v']
                        store(VV[w][v0:v0 + 512, :].rearrange("(t p) n -> p t n", p=128), stv,
                              r=[('stv', tt) for tt in range(4)], w=[('stv', tt) for tt in range(4)])
        P.barrier()

    if stop_after >= 2:
        ar.reset()
        kTb = [ar.bf16(4096) for _ in range(2)]
        qTb = [ar.bf16(2048) for _ in range(2)]
        NCH = {1: 17, 4: 20, 16: 32}
        Vh = [{d: ar.bf16(NCH[d], 128) for d in (1, 4, 16)} for _ in range(2)]
        accden = [ar.f32(2, 2048) for _ in range(2)]
        sbt = [ar.f32(2, 128) for _ in range(4)]
        pT = [ar.bf16(2, 128) for _ in range(4)]
        ast = [ar.bf16(2048) for _ in range(2)]
        sqb = [ar.bf16(2048) for _ in range(2)]
        for i in range(2):
            P.add('dve', lambda e, i=i: e.memset(kTb[i], 0.0), w=[('kT', i)])
        wa_bg = [ar.bf16(16, 512) for _ in range(2)]
        gb = ([ar.f32(512) for _ in range(2)], [ar.f32(512) for _ in range(2)], [ar.f32(512) for _ in range(2)])
        wo32 = [ar.f32(2048) for _ in range(2)]
        wo16 = [ar.bf16(2048) for _ in range(2)]
        bg_tasks = []

        def t_mod(j):
            wj = wa_bg[j % 2]
            kw = ('wabg', j % 2)
            load(wj, w_ada_v[j], w=[kw], eng='pool')
            mod_block(j, wj, kw, j % 2, 2 + j % 2, gb)

        def t_wo(kc):
            i = kc % 2
            load(wo32[i], w_out[kc * 128:(kc + 1) * 128, :], w=[('wo32', i)])
            P.add('dve', lambda e: e.tensor_scalar(out=wo16[i], in0=wo32[i], scalar1=gcols[:, 2, kc:kc + 1],
                                                   scalar2=None, op0=ALU.mult),
                  r=[('wo32', i), 'gcols'], w=[('wo16', i)])
            store(Wb_out[:, :, kc, :].rearrange("nb p n -> p nb n"), wo16[i].rearrange("p (a b) -> p a b", b=512),
                  r=[('wo16', i)], w=[('wo16', i)])
        for i in range(16):
            bg_tasks.append((t_mod, 8 + i))
            bg_tasks.append((t_wo, i))
        hb = 0
        tile_ctr = 0
        tbase = 0
        for w in WINS:
            S = SOWN[w]
            sample = (w == 2)
            ntt = S // 128
            tbase = 16 * w
            for h in range(8):
                bi = hb % 2
                hb += 1
                kT, qT, V_, AD = kTb[bi], qTb[bi], Vh[bi], accden[bi]
                vz_keys = [('VVz', w, r0) for r0 in ((list(range(0, 1024, 256)) + list(range(3072, 4096, 256))) if w < 2 else list(range(3072, 4096, 256)))]
                if sample:
                    load(kT[:, 0:3072], KT[w][h], w=[('kT', bi)])
                else:
                    load(kT[:, 1024:3072], KT[w][h], w=[('kT', bi)])
                load(qT[:, 0:S], QT[w][h], w=[('qT', bi)])
                for d in (1, 4, 16):
                    L = S // d
                    Lh = 1024 // d
                    nch = (L + 128 + 127) // 128
                    for r_ in range(d):
                        t0 = (Lh - 64) * d + r_
                        src = VV[w][t0:t0 + (nch * 128 - 1) * d + 1:d, h * 128:(h + 1) * 128].rearrange("(m k) n -> k m n", k=128)
                        load(V_[d][:, r_ * nch:(r_ + 1) * nch, :], src, w=[('V', bi, d, r_)], r=vz_keys)
                tiles = []
                for bidx, d in enumerate((1, 4, 16)):
                    L = S // d
                    nq = min(128, L)
                    for r_ in range(d):
                        for n in range(L // nq):
                            tiles.append((bidx, d, r_, n))
                LA = 3

                def stage1(tl, ti):
                    bidx, d, r_, n = tl
                    L = S // d
                    Lh = 1024 // d
                    nq = min(128, L)
                    ntile = L // nq
                    coef = -(2.0 ** -(h + 1)) * d / SC
                    pS = ps[ti]
                    kS = ('ps', ti)
                    qcols = slice(r_ + 128 * n * d, r_ + 128 * n * d + (nq - 1) * d + 1, d)
                    qap = qT[:, qcols]
                    kT_ = kT

                    def fn(pe, kT=kT_):
                        ins = None
                        for c in range(2):
                            k0 = (Lh - 64 + 128 * (n + c)) * d + r_
                            ins = pe.matmul(pS[:, c * 128:c * 128 + nq], kT[:, k0:k0 + 127 * d + 1:d], qap,
                                            start=True, stop=True)
                        return ins
                    P.add('pe', fn, r=[('kT', bi), ('qT', bi)], w=[kS])
                    sbv = sbt[ti][:, :, 0:nq]
                    P.add('dve', lambda e: e.scalar_tensor_tensor(
                        out=sbv, in0=dm[:, :].rearrange("p (c n) -> p c n", c=2)[:, :, 0:nq], scalar=coef,
                        in1=pS[:, 0:256].rearrange("p (c n) -> p c n", c=2)[:, :, 0:nq], op0=ALU.mult, op1=ALU.add),
                        r=[kS, 'dm'], w=[('sb', ti)])
                    ecol = [0, 0]
                    if n == 0:
                        ecol[0] = 3 if sample else 1
                    if n == ntile - 1:
                        if sample:
                            ecol[1] = 5 if d == 16 else 4
                        else:
                            ecol[1] = 2
                    pTv = pT[ti][:, :, 0:nq]
                    if ecol == [0, 0]:
                        P.add('act', lambda e: e.activation(out=pTv, in_=sbv, func=AF.Exp, scale=SC),
                              r=[('sb', ti)], w=[('pT', ti)])
                    else:
                        for c in range(2):
                            P.add('act', lambda e, c=c, ec=ecol[c]: e.activation(
                                out=pTv[:, c, :], in_=sbv[:, c, :], func=AF.Exp, scale=SC, bias=edge[:, ec:ec + 1]),
                                r=[('sb', ti), 'edge'], w=[('pT', ti)])

                def stage2(tl, ti):
                    bidx, d, r_, n = tl
                    L = S // d
                    nch = (L + 128 + 127) // 128
                    nq = min(128, L)
                    pO = ps[4 + ti]
                    kO = ('ps', 4 + ti)
                    pTv = pT[ti][:, :, 0:nq]
                    qcols = slice(r_ + 128 * n * d, r_ + 128 * n * d + (nq - 1) * d + 1, d)

                    Vd = V_[d]

                    def fn2(pe):
                        ins = None
                        for c in range(2):
                            ins = pe.matmul(pO[:, 0:nq], Vd[:, r_ * nch + n + c, :], pTv[:, c, :],
                                            start=(c == 0), stop=(c == 1))
                        for c in range(2):
                            ins = pe.matmul(pO[:, 128:128 + nq], onesb[:, :], pTv[:, c, :],
                                            start=(c == 0), stop=(c == 1))
                        return ins
                    P.add('pe', fn2, r=[('V', bi, d, r_), ('pT', ti), 'onesb'], w=[kO])
                    oap = AD[:, :, qcols]
                    iap = pO[:, 0:256].rearrange("p (c n) -> p c n", c=2)[:, :, 0:nq]
                    if bidx == 0:
                        P.add('act', lambda e: e.activation(out=oap, in_=iap, func=AF.Copy),
                              r=[kO], w=[('AD', bi)])
                    else:
                        P.add('dve', lambda e: e.tensor_tensor(out=oap, in0=iap, in1=oap, op=ALU.add),
                              r=[kO], w=[('AD', bi)])

                slots = []
                for s_ in range(len(tiles) + LA):
                    if s_ < len(tiles):
                        ti = tile_ctr % 4
                        tile_ctr += 1
                        slots.append(ti)
                        stage1(tiles[s_], ti)
                    if s_ - LA >= 0:
                        stage2(tiles[s_ - LA], slots[s_ - LA])
                P.add('act', lambda e, AD=AD, S=S: e.activation(out=AD[:, 1, 0:S], in_=AD[:, 1, 0:S], func=AF.Ln), r=[], w=[('AD', bi)])
                P.add('act', lambda e, AD=AD, S=S: e.activation(out=AD[:, 1, 0:S], in_=AD[:, 1, 0:S], func=AF.Exp, scale=-1.0), r=[], w=[('AD', bi)])
                P.add('dve', lambda e, AD=AD, S=S: e.tensor_tensor(out=AD[:, 0, 0:S], in0=AD[:, 0, 0:S], in1=AD[:, 1, 0:S], op=ALU.mult),
                      r=[], w=[('AD', bi)])
                P.add('act', lambda e, AD=AD, S=S, bi=bi: e.activation(out=ast[bi][:, 0:S], in_=AD[:, 0, 0:S], func=AF.Copy),
                      r=[('AD', bi)], w=[('ast', bi)])
                P.add('dve', lambda e, AD=AD, S=S, bi=bi: e.tensor_tensor(out=sqb[bi][:, 0:S], in0=AD[:, 0, 0:S], in1=AD[:, 0, 0:S], op=ALU.mult),
                      r=[('AD', bi)], w=[('sqb', bi)])
                store(AT[w][h * 128:(h + 1) * 128, :], ast[bi][:, 0:S], r=[('ast', bi)], w=[('ast', bi)])
                for _ in range(2):
                    if bg_tasks:
                        f_, a_ = bg_tasks.pop(0)
                        f_(a_)

                def fn3(pe, bi=bi, ntt=ntt):
                    ins = None
                    for tt in range(ntt):
                        ins = pe.matmul(ps[4][:, tt:tt + 1], sqb[bi][:, tt * 128:(tt + 1) * 128], onesb[:, 0:1],
                                        start=True, stop=True)
                    return ins
                P.add('pe', fn3, r=[('sqb', bi), 'onesb'], w=[('ps', 4)])
                if h == 0:
                    P.add('dve', lambda e, ntt=ntt, tbase=tbase: e.tensor_copy(out=ssqa[:, tbase:tbase + ntt], in_=ps[4][:, 0:ntt]),
                          r=[('ps', 4)], w=['ssqa'])
                else:
                    P.add('dve', lambda e, ntt=ntt, tbase=tbase: e.tensor_tensor(out=ssqa[:, tbase:tbase + ntt], in0=ps[4][:, 0:ntt],
                                                                                   in1=ssqa[:, tbase:tbase + ntt], op=ALU.add),
                          r=[('ps', 4)], w=['ssqa'])
            tbase += ntt
        while bg_tasks:
            f_, a_ = bg_tasks.pop(0)
            f_(a_)
        mod_finish(3, 4, 2, 3, 1, 'cols2')
        P.barrier()

    if stop_after >= 3:
        ar.reset()
        fst = [ar.bf16(512) for _ in range(8)]
        fsq = [ar.bf16(512) for _ in range(8)]
        base = ar.off
        tbase = 0
        pass_ctr = 0
        slc = 0
        for w in WINS:
            S = SOWN[w]
            ntt = S // 128
            nj = S // 512
            tbase = 16 * w
            ar.off = base
            if w < 2:
                pqr = ar.bf16(16, 2048)
                tb = [ar.bf16(2, 16, 512) for _ in range(2)]
                for ch in range(16):
                    load(pqr[:, ch, :], PQ[w][ch * 128:(ch + 1) * 128, :], w=[('pqr', ch)])
                pq5 = pqr.rearrange("p c (e q n) -> p c e q n", e=8, q=2)
            else:
                P.barrier()
                pqs = [ar.bf16(8, 1024) for _ in range(2)]
                tbs = [ar.bf16(2, 8, 512) for _ in range(2)]
            for j in range(nj):
                if w < 2:
                    tj = tb[j % 2]
                    tk = ('tb', j % 2)
                    tpv = tabp[j].rearrange("p (a b c) -> p a b c", a=2, b=16)
                    for cs_ in range(2):
                        for hh in range(2):
                            load(tj[:, cs_, hh * 8:(hh + 1) * 8, :], tpv[:, cs_, hh * 8:(hh + 1) * 8, :], w=[tk])
                for half in range(2):
                    bset = (pass_ctr % 2) * 4
                    pass_ctr += 1
                    if w < 2:
                        for e4 in range(4):
                            et = half * 4 + e4
                            pi = bset + e4
                            pairs = []
                            for ch in range(16):
                                pairs.append((pq5[:, ch, et, 0, :], tj[:, 0, ch, :]))
                                pairs.append((pq5[:, ch, et, 1, :], tj[:, 1, ch, :]))
                            mm_group(ps[pi][:, :], pairs, r=[('pqr', ch) for ch in range(16)] + [tk], w=[('ps', pi)])
                    else:
                        for grp in range(8):
                            i = slc % 2
                            slc += 1
                            src_ = PQ[2][grp * 1024:(grp + 1) * 1024, half * 1024:(half + 1) * 1024].rearrange("(c p) n -> p c n", p=128)
                            load(pqs[i], src_, w=[('pqs', i)])
                            load(tbs[i], tabs[j, grp].rearrange("p (a b c) -> p a b c", a=2, b=8), w=[('tbs', i)])
                            pv = pqs[i].rearrange("p c (e q n) -> p c e q n", e=4, q=2)
                            for e4 in range(4):
                                pi = bset + e4

                                def fn(pe, pv=pv, tt_=tbs[i], e4=e4, pi=pi, grp=grp):
                                    ins = None
                                    for ch in range(8):
                                        for q_ in range(2):
                                            ins = pe.matmul(ps[pi][:, :], pv[:, ch, e4, q_, :], tt_[:, q_, ch, :],
                                                            start=(grp == 0 and ch == 0 and q_ == 0),
                                                            stop=(grp == 7 and ch == 7 and q_ == 1))
                                    return ins
                                P.add('pe', fn, r=[('pqs', i), ('tbs', i)], w=[('ps', pi)])
                    for e4 in range(4):
                        et = half * 4 + e4
                        pi = bset + e4
                        P.add('dve', lambda e, et=et, pi=pi: e.tensor_copy(out=fst[et], in_=ps[pi][:, :]),
                              r=[('ps', pi)], w=[('fst', et)])
                        P.add('act', lambda e, et=et, pi=pi: e.activation(out=fsq[et], in_=fst[et], func=AF.Square),
                              r=[('fst', et)], w=[('fsq', et)])
                        store(FT[w][et * 128:(et + 1) * 128, j * 512:(j + 1) * 512], fst[et], r=[('fst', et)], w=[('fst', et)])
                pq_ = bset

                def fnq(pe, pq_=pq_):
                    ins = None
                    for tt in range(4):
                        for et in range(8):
                            ins = pe.matmul(ps[pq_][:, tt:tt + 1], fsq[et][:, tt * 128:(tt + 1) * 128], onesb[:, 0:1],
                                            start=(et == 0), stop=(et == 7))
                    return ins
                P.add('pe', fnq, r=[('fsq', et) for et in range(8)] + ['onesb'], w=[('ps', pq_)])
                c0 = tbase + j * 4
                P.add('dve', lambda e, pq_=pq_, c0=c0: e.tensor_copy(out=ssqf[:, c0:c0 + 4], in_=ps[pq_][:, 0:4]),
                      r=[('ps', pq_)], w=['ssqf'])
            tbase += ntt
        P.barrier()

    if stop_after >= 4:
        ar.reset()
        NT = 40
        for (src_, dst_, nm) in ((ssqa, rsa, 'rsa'), (ssqf, rsf, 'rsf')):
            P.add('act', lambda e, src_=src_, dst_=dst_: e.activation(out=dst_[:, 0:NT], in_=src_[:, 0:NT], func=AF.Sqrt, scale=1.0 / 1024, bias=epsc[:, 0:1]), w=[nm])
            P.add('dve', lambda e, dst_=dst_: e.reciprocal(out=dst_[:, 0:NT], in_=dst_[:, 0:NT]), r=[nm], w=[nm])
        xt = [ar.f32(2048) for _ in range(4)]
        r1o = ar.off
        aT = ar.bf16(8, 512)
        fT = ar.bf16(8, 512)
        gp1 = ar.f32(2048)
        y1 = [ar.f32(2048) for _ in range(4)]
        tmpA = [ar.f32(512) for _ in range(2)]
        ar.off = r1o
        hid = [ar.bf16(512) for _ in range(64)]

        def R1(a, n):
            return [('R1', i) for i in range(a, a + n)]
        k_aT, k_fT, k_gp1 = R1(0, 8), R1(8, 8), R1(16, 8)
        k_y1 = [R1(24 + 8 * t, 8) for t in range(4)]
        k_tmpA = [R1(56, 2), R1(58, 2)]
        y2 = [ar.f32(2048) for _ in range(4)]
        xs2 = [y2[t][:, 0:1024].bitcast(BF16) for t in range(4)]
        WS = [ar.bf16(8192) for _ in range(2)]
        h2o = ar.off
        h2T = [ar.bf16(512) for _ in range(16)]
        ar.off = h2o
        y2T = [ar.f32(512) for _ in range(4)]
        ar.off = h2o + 4096
        gp2 = ar.f32(2048)
        rtmp = [ar.f32(512) for _ in range(2)]
        st3 = ar.f32(16)
        wsc = [0]
        psr3 = [0]
        rl = [0]
        blocks3 = []
        tb_ = {0: 0, 1: 16, 2: 32}
        for w in WINS:
            for k in range(SOWN[w] // 512):
                blocks3.append((w, k * 512))

        def ws_load(src_ap, shape3, extra_r=()):
            i = wsc[0] % 2
            wsc[0] += 1
            v = WS[i].rearrange("p (a b) -> p a b", b=shape3[1])
            load(v, src_ap, w=[('WS', i)], r=list(extra_r))
            return v, ('WS', i)

        for (w, tok0) in blocks3:
            b = w
            tile0 = tb_[w] + tok0 // 128
            load(aT, AT[w][:, tok0:tok0 + 512].rearrange("(kc p) n -> p kc n", p=128), w=k_aT)
            load(fT, FT[w][:, tok0:tok0 + 512].rearrange("(kc p) n -> p kc n", p=128), w=k_fT)
            pre_w = ws_load(Wb_out[0], (16, 512))
            load(gp1, GP[b, 0:1, :].partition_broadcast(128)[:, 0, :], w=k_gp1)
            for t in range(4):
                if w < 2:
                    xsrc = xp[w, tok0 + t * 128:tok0 + (t + 1) * 128, :]
                else:
                    xsrc = xsw[1024 + tok0 + t * 128:1024 + tok0 + (t + 1) * 128, :]
                load(xt[t], xsrc, w=[('xt', t)])
            load(gp2, GP[b, 1:2, :].partition_broadcast(128)[:, 0, :], w=['gp2'])
            for nb in range(4):
                wv, wk = pre_w if nb == 0 else ws_load(Wb_out[nb], (16, 512))
                for tt in range(4):
                    pa = psr3[0] % 8
                    pb = (psr3[0] + 1) % 8
                    psr3[0] += 2
                    mm_group(ps[pa][:, :], [(aT[:, kc, tt * 128:(tt + 1) * 128], wv[:, kc, :]) for kc in range(8)],
                             r=k_aT + [wk], w=[('ps', pa)])
                    mm_group(ps[pb][:, :], [(fT[:, kc, tt * 128:(tt + 1) * 128], wv[:, 8 + kc, :]) for kc in range(8)],
                             r=k_fT + [wk], w=[('ps', pb)])
                    ti = (nb * 4 + tt) % 2
                    tl = tile0 + tt
                    P.add('act', lambda e, ti=ti, pa=pa, tl=tl: e.activation(out=tmpA[ti], in_=ps[pa][:, :], func=AF.Copy,
                                                                            scale=rsa[:, tl:tl + 1]),
                          r=[('ps', pa), 'rsa'], w=k_tmpA[ti])
                    P.add('dve', lambda e, ti=ti, pb=pb, tl=tl, tt=tt, nb=nb: e.scalar_tensor_tensor(
                        out=y1[tt][:, nb * 512:(nb + 1) * 512], in0=ps[pb][:, :], scalar=rsf[:, tl:tl + 1], in1=tmpA[ti],
                        op0=ALU.mult, op1=ALU.add),
                        r=[('ps', pb), 'rsf'] + k_tmpA[ti], w=k_y1[tt])
            for t in range(4):
                P.add('act', lambda e, t=t: e.activation(out=y2[t], in_=y1[t], func=AF.Square, accum_out=st3[:, t:t + 1]),
                      r=k_y1[t], w=[('y2', t), ('st3', t)])
                P.add('act', lambda e, t=t: e.activation(out=st3[:, t:t + 1], in_=st3[:, t:t + 1], func=AF.Sqrt, scale=1.0 / 2048, bias=epsc[:, 0:1]), r=[], w=[('st3', t)])
                P.add('dve', lambda e, t=t: e.reciprocal(out=st3[:, t:t + 1], in_=st3[:, t:t + 1]), r=[], w=[('st3', t)])
                P.add('dve', lambda e, t=t: e.scalar_tensor_tensor(out=y1[t], in0=y1[t], scalar=st3[:, t:t + 1], in1=gp1,
                                                                   op0=ALU.mult, op1=ALU.mult),
                      r=k_gp1 + [('st3', t)], w=k_y1[t])
                P.add('pool' if t % 2 == 0 else 'dve', lambda e, t=t: e.tensor_tensor(out=xt[t], in0=y1[t], in1=xt[t], op=ALU.add),
                      r=k_y1[t], w=[('xt', t)])
            for t in range(4):
                P.add('act', lambda e, t=t: e.activation(out=y2[t], in_=xt[t], func=AF.Square, accum_out=st3[:, 4 + t:5 + t]),
                      r=[('xt', t)], w=[('y2', t), ('st3b', t)])
                P.add('act', lambda e, t=t: e.activation(out=st3[:, 4 + t:5 + t], in_=st3[:, 4 + t:5 + t], func=AF.Sqrt, scale=1.0 / 2048, bias=epsc[:, 0:1]), r=[], w=[('st3b', t)])
                P.add('dve', lambda e, t=t: e.reciprocal(out=st3[:, 4 + t:5 + t], in_=st3[:, 4 + t:5 + t]), r=[], w=[('st3b', t)])
                P.add('dve', lambda e, t=t: e.tensor_scalar(out=xs2[t], in0=xt[t], scalar1=st3[:, 4 + t:5 + t], scalar2=None,
                                                            op0=ALU.mult),
                      r=[('xt', t), ('st3b', t)], w=[('y2', t)])
            for dc in range(16):
                pi = dc % 4

                def fn(pe, dc=dc, pi=pi):
                    ins = None
                    for t in range(4):
                        ins = pe.transpose(psb[pi][:, t * 128:(t + 1) * 128], xs2[t][:, dc * 128:(dc + 1) * 128], identb[:, :])
                    return ins
                P.add('pe', fn, r=[('y2', t) for t in range(4)] + ['identb'], w=[('ps', pi)])
                P.add('act', lambda e, dc=dc, pi=pi, b=b: e.activation(out=h2T[dc], in_=psb[pi][:, 0:512], func=AF.Identity,
                                                                     bias=cols[:, 3, dc, b:b + 1], scale=cols[:, 2, dc, b:b + 1]),
                      r=[('ps', pi), 'cols2'], w=[('h2T', dc)])
            h2keys = [('h2T', dc) for dc in range(16)]
            for fg in range(16):
                wv, wk = ws_load(Wb_mi[fg], (16, 512), extra_r=[('Wb_mi', q_) for q_ in range(16)])
                for t in range(4):
                    ft = fg * 4 + t
                    pi = 4 + psr3[0] % 4
                    psr3[0] += 1
                    mm_group(ps[pi][:, :], [(wv[:, dc, t * 128:(t + 1) * 128], h2T[dc]) for dc in range(16)],
                             r=h2keys + [wk], w=[('ps', pi)])
                    ri = rl[0] % 2
                    rl[0] += 1
                    P.add('act', lambda e, ri=ri, pi=pi: e.activation(out=rtmp[ri], in_=ps[pi][:, :], func=AF.Relu),
                          r=[('ps', pi)], w=[('rtmp', ri)])
                    if ft % 2 == 0:
                        P.add('dve', lambda e, ri=ri, ft=ft: e.tensor_tensor(out=hid[ft], in0=rtmp[ri], in1=rtmp[ri], op=ALU.mult),
                              r=[('rtmp', ri)], w=R1(ft, 1))
                    else:
                        P.add('pool', lambda e, ri=ri, ft=ft: e.tensor_tensor(out=hid[ft], in0=rtmp[ri], in1=rtmp[ri], op=ALU.mult),
                              r=[('rtmp', ri)], w=R1(ft, 1))
            hkeys = R1(0, 64)
            mo_keys = [('Wb_mo', q_) for q_ in range(16)]
            for nb in range(4):
                bb = 4 * (nb % 2)
                for pcs in range(4):
                    wv, wk = ws_load(Wb_mo[nb, pcs], (16, 512), extra_r=mo_keys)
                    for tt in range(4):
                        def fnm(pe, wv=wv, tt=tt, pcs=pcs, bb=bb):
                            ins = None
                            for f16 in range(16):
                                ins = pe.matmul(ps[bb + tt][:, :], hid[pcs * 16 + f16][:, tt * 128:(tt + 1) * 128], wv[:, f16, :],
                                                start=(pcs == 0 and f16 == 0), stop=(pcs == 3 and f16 == 15))
                            return ins
                        P.add('pe', fnm, r=R1(pcs * 16, 16) + [wk], w=[('ps', bb + tt)])
                for tt in range(4):
                    evac_copy(y2[tt][:, nb * 512:(nb + 1) * 512], ps[bb + tt][:, :], r=[('ps', bb + tt)], w=[('y2', tt)])
            for t in range(4):
                P.add('act', lambda e, t=t: e.activation(out=y1[t], in_=y2[t], func=AF.Square, accum_out=st3[:, 8 + t:9 + t]),
                      r=[('y2', t)], w=k_y1[t] + [('st3c', t)])
                P.add('act', lambda e, t=t: e.activation(out=st3[:, 8 + t:9 + t], in_=st3[:, 8 + t:9 + t], func=AF.Sqrt, scale=1.0 / 2048, bias=epsc[:, 0:1]), r=[], w=[('st3c', t)])
                P.add('dve', lambda e, t=t: e.reciprocal(out=st3[:, 8 + t:9 + t], in_=st3[:, 8 + t:9 + t]), r=[], w=[('st3c', t)])
                P.add('dve', lambda e, t=t: e.scalar_tensor_tensor(out=y2[t], in0=y2[t], scalar=st3[:, 8 + t:9 + t], in1=gp2,
                                                                   op0=ALU.mult, op1=ALU.mult),
                      r=['gp2', ('st3c', t)], w=[('y2', t)])
                P.add('pool' if t % 2 == 0 else 'dve', lambda e, t=t: e.tensor_tensor(out=y2[t], in0=y2[t], in1=xt[t], op=ALU.add),
                      r=[('xt', t)], w=[('y2', t)])
                if w < 2:
                    dst = yp[w, tok0 + t * 128:tok0 + (t + 1) * 128, :]
                else:
                    dst = ys[tok0 + t * 128:tok0 + (t + 1) * 128, :]
                store(dst, y2[t], r=[('y2', t)], w=[('y2', t)], eng=('pool' if t % 2 == 0 else 'act'))
    return nc, P, stack


_CONST = {}


def _bf(a):
    return np.ascontiguousarray(a.astype(np.float32)).astype(ml_dtypes.bfloat16)


def _consts():
    if _CONST:
        return _CONST
    C = _CONST
    C['identf'] = np.eye(128, dtype=np.float32)
    C['identb'] = _bf(np.eye(128))
    kk = np.arange(128)[:, None]
    ii = np.arange(128)[None, :]
    d0 = np.where(kk >= ii, np.abs(ii + 64 - kk), DBIG).astype(np.float32)
    d1 = np.where(kk <= ii, np.abs(ii - 64 - kk), DBIG).astype(np.float32)
    C['dmask'] = np.ascontiguousarray(np.concatenate([d0, d1], axis=1))
    cp = (np.arange(2)[None, :, None] * 128 + np.arange(128)[:, None, None])
    c = np.arange(256)[None, None, :]
    ang = 2 * np.pi * ((cp * c) % 256) / 256.0
    tc = np.stack([np.cos(ang) / 16.0, -np.sin(ang) / 16.0], axis=1)
    C['tabc'] = _bf(tc.reshape(128, 1024))
    S = 2048
    lut_c = np.cos(2 * np.pi * np.arange(S) / S) / np.sqrt(S)
    lut_s = np.sin(2 * np.pi * np.arange(S) / S) / np.sqrt(S)
    s = (np.arange(16)[None, :, None] * 128 + np.arange(128)[:, None, None])
    tabp = np.empty((4, 128, 2, 16, 512), dtype=np.float32)
    for j in range(4):
        sp_ = j * 512 + np.arange(512)[None, None, :]
        k = (s * sp_) % S
        tabp[j, :, 0] = lut_c[k]
        tabp[j, :, 1] = lut_s[k]
    C['tabp'] = _bf(tabp.reshape(4, 128, 2 * 16 * 512))
    S = 8192
    lut_c = (np.cos(2 * np.pi * np.arange(S) / S) / np.sqrt(S)).astype(np.float32)
    lut_s = (np.sin(2 * np.pi * np.arange(S) / S) / np.sqrt(S)).astype(np.float32)
    tabs_all = []
    for core in range(8):
        t = np.empty((2, 8, 128, 2, 8, 512), dtype=np.float32)
        for j in range(2):
            sp_ = 1024 * core + j * 512 + np.arange(512, dtype=np.int64)[None, None, :]
            for grp in range(8):
                s = ((grp * 8 + np.arange(8, dtype=np.int64))[None, :, None] * 128 + np.arange(128, dtype=np.int64)[:, None, None])
                k = (s * sp_) % S
                t[j, grp, :, 0] = lut_c[k]
                t[j, grp, :, 1] = lut_s[k]
        tabs_all.append(_bf(t.reshape(2, 8, 128, 2 * 8 * 512)))
    C['tabs'] = tabs_all
    edges = []
    for core in range(8):
        vl = NEG if core == 0 else 0.0
        vr = NEG if core == 7 else 0.0
        e = np.zeros((128, 8), dtype=np.float32)
        e[:64, 1] = NEG
        e[64:, 2] = NEG
        e[:64, 3] = vl
        e[64:, 4] = vr
        e[:64, 5] = vr
        e[64:, 5] = NEG
        edges.append(e)
    C['edge'] = edges
    return C


def make_in_maps(inputs, n_cores=8):
    C = _consts()
    f32 = lambda a: np.ascontiguousarray(np.asarray(a, dtype=np.float32))
    x_prompt = np.asarray(inputs['x_prompt'], dtype=np.float32)
    x_sample = f32(np.asarray(inputs['x_sample'])[0])
    c_prompt = np.asarray(inputs['c_prompt'], dtype=np.float32)
    c_sample = np.asarray(inputs['c_sample'], dtype=np.float32)
    w_ada = f32(np.asarray(inputs['w_ada'])[0])
    b_ada = np.asarray(inputs['b_ada'], dtype=np.float32)[0]
    shared = {
        'xsf': x_sample,
        'w_ada': w_ada,
        'b_ada_col': f32(b_ada.reshape(96, 128).T),
        'b_ada_row': f32(b_ada.reshape(1, 12288)),
        'w_in': f32(np.asarray(inputs['w_in'])[0]),
        'w_four': f32(np.asarray(inputs['w_fourier'])[0]),
        'w_out': f32(np.asarray(inputs['w_out'])[0]),
        'w_mi': f32(np.asarray(inputs['w_mlp_in'])[0]),
        'w_mo': f32(np.asarray(inputs['w_mlp_out'])[0]),
        'tabc': C['tabc'], 'tabp': C['tabp'], 'dmask': C['dmask'],
        'identf': C['identf'], 'identb': C['identb'],
    }
    gout = np.concatenate([np.asarray(inputs['g_attn_out'], dtype=np.float32)[0],
                           np.asarray(inputs['g_fourier_out'], dtype=np.float32)[0]])
    gc = np.stack([np.asarray(inputs['g_pre_mix'], dtype=np.float32)[0].reshape(16, 128).T,
                   np.asarray(inputs['g_pre_mlp'], dtype=np.float32)[0].reshape(16, 128).T,
                   gout.reshape(16, 128).T], axis=1)
    shared['gcols'] = f32(gc.reshape(128, 48))
    shared['grows'] = f32(np.stack([np.asarray(inputs['g_post_mix'], dtype=np.float32)[0],
                                    np.asarray(inputs['g_post_mlp'], dtype=np.float32)[0]]))
    maps = []
    for core in range(n_cores):
        m = dict(shared)
        m['xp'] = np.ascontiguousarray(x_prompt[2 * core:2 * core + 2])
        xw = np.zeros((3072, 2048), dtype=np.float32)
        lo, hi = 1024 * (core - 1), 1024 * (core + 2)
        a, b = max(lo, 0), min(hi, 8192)
        xw[a - lo:b - lo] = x_sample[a:b]
        m['xsw'] = xw
        cs = np.stack([c_prompt[2 * core], c_prompt[2 * core + 1], c_sample[0]], axis=1)
        m['ccol'] = f32(cs.reshape(16, 128, 3).transpose(1, 0, 2).reshape(128, 48))
        m['tabs'] = C['tabs'][core]
        m['edge'] = C['edge'][core]
        maps.append(m)
    return maps


_NC = {}


def get_nc(cfg=None):
    key = repr(cfg)
    if key not in _NC:
        nc, P, stack = build(cfg)
        P.emit(nc, stack)
        stack.close()
        _NC[key] = nc
    return _NC[key]


def kernel(**inputs):
    maps = make_in_maps(inputs)
    nc = get_nc(None)
    res = run_bass_kernel_spmd(nc, maps, core_ids=list(range(8)))
    y_prompt = np.empty((16, 2048, 2048), dtype=np.float32)
    y_sample = np.empty((1, 8192, 2048), dtype=np.float32)
    for core in range(8):
        r = res.results[core]
        y_prompt[2 * core:2 * core + 2] = np.asarray(r['yp'], dtype=np.float32)
        y_sample[0, 1024 * core:1024 * (core + 1)] = np.asarray(r['ys'], dtype=np.float32)
    return (y_prompt, y_sample)
```

```python
import numpy as np
import ml_dtypes
from contextlib import ExitStack
import concourse.bass as bass
import concourse.mybir as mybir
from concourse.bass_utils import run_bass_kernel_spmd

F32 = mybir.dt.float32
BF16 = mybir.dt.bfloat16
AF = mybir.ActivationFunctionType
ALU = mybir.AluOpType
AX = mybir.AxisListType

NEG = -30000.0
EPS = 1e-6
SC = float(128 ** -0.5)
DBIG = 1.0e9
ENGS = ('pe', 'act', 'dve', 'pool', 'sp')
NDS = 8


class Op:
    __slots__ = ('eng', 'fn', 'deps', 'dma', 'ms', 'dj', 'sig')

    def __init__(self, eng, fn, deps, dma):
        self.eng = eng
        self.fn = fn
        self.deps = deps
        self.dma = dma
        self.ms = 0
        self.dj = -1
        self.sig = False


class Prog:
    def __init__(self):
        self.ops = []
        self.lw = {}
        self.rd = {}
        self.bar = set()
        self.bar_pending = set()
        self.last_on = {}
        self.all_dma = []

    def add(self, eng, fn, r=(), w=(), dma=False, nobar=False):
        deps = set()
        for k in r:
            i = self.lw.get(k)
            if i is not None:
                deps.add(i)
        for k in w:
            i = self.lw.get(k)
            if i is not None:
                deps.add(i)
            deps.update(self.rd.get(k, ()))
        if eng in self.bar_pending:
            deps.update(self.bar)
            self.bar_pending.discard(eng)
        idx = len(self.ops)
        self.ops.append(Op(eng, fn, deps, dma))
        for k in r:
            self.rd.setdefault(k, []).append(idx)
        for k in w:
            self.lw[k] = idx
            self.rd[k] = []
        if not nobar:
            self.last_on[eng] = idx
        if dma and not nobar:
            self.all_dma.append(idx)
        return idx

    def barrier(self):
        self.bar = set(self.last_on.values()) | set(self.all_dma)
        self.bar_pending = set(ENGS)

    def emit(self, nc, stack):
        ops = self.ops
        for op in ops:
            for d in op.deps:
                ops[d].sig = True
        cnt = {e: 0 for e in ENGS}
        dcnt = {e: 0 for e in ENGS}
        dlist = {e: [] for e in ENGS}
        for i, op in enumerate(ops):
            if op.dma:
                op.dj = dcnt[op.eng]
                dcnt[op.eng] += 1
                dlist[op.eng].append(i)
            elif op.sig:
                cnt[op.eng] += 1
                op.ms = cnt[op.eng]
        for e in ENGS:
            assert cnt[e] < 60000, (e, cnt[e])
        sem = {e: stack.enter_context(nc.semaphore("s_" + e)) for e in ENGS}
        dsem = {e: [stack.enter_context(nc.semaphore("d_%s%d" % (e, i))) for i in range(NDS)]
                for e in ENGS if dcnt[e] > 0}

        def completion(i):
            op = ops[i]
            if op.dma:
                return dsem[op.eng][op.dj % NDS], 16 * (op.dj // NDS + 1)
            return sem[op.eng], op.ms

        per = {e: [i for i, op in enumerate(ops) if op.eng == e] for e in ENGS}
        final = {}
        for e in ENGS:
            if cnt[e] > 0:
                final[sem[e]] = cnt[e]
            for j in range(dcnt[e]):
                s = dsem[e][j % NDS]
                final[s] = max(final.get(s, 0), 16 * (j // NDS + 1))

        def run(e, eng):
            seen = {}
            for i in per[e]:
                op = ops[i]
                waits = {}
                for d in op.deps:
                    if e == 'pe' and ops[d].eng == 'pe':
                        continue
                    s, v = completion(d)
                    if waits.get(s, 0) < v:
                        waits[s] = v
                if op.dma and op.dj >= NDS:
                    s, v = completion(dlist[e][op.dj - NDS])
                    if waits.get(s, 0) < v:
                        waits[s] = v
                for s, v in waits.items():
                    if seen.get(s, 0) < v:
                        eng.wait_ge(s, v)
                        seen[s] = v
                ins = op.fn(eng)
                if op.dma:
                    ins.then_inc(dsem[e][op.dj % NDS], 16)
                elif op.sig:
                    ins.then_inc(sem[e], 1)
            if e == 'sp':
                for s, v in final.items():
                    if seen.get(s, 0) < v:
                        eng.wait_ge(s, v)

        with nc.Block() as block:
            @block.tensor
            def _(eng):
                run('pe', eng)

            @block.scalar
            def _(eng):
                run('act', eng)

            @block.vector
            def _(eng):
                run('dve', eng)

            @block.gpsimd
            def _(eng):
                run('pool', eng)

            @block.sync
            def _(eng):
                run('sp', eng)


class Arena:
    def __init__(self, t, n):
        self.t = t
        self.n = n
        self.off = 0

    def reset(self):
        self.off = 0

    def f32(self, *shape):
        n = int(np.prod(shape))
        a = self.t[:, self.off:self.off + n]
        self.off += n
        assert self.off <= self.n, ("arena overflow", self.off, self.n)
        if len(shape) == 2:
            a = a.rearrange("p (a b) -> p a b", b=shape[1])
        elif len(shape) == 3:
            a = a.rearrange("p (a b c) -> p a b c", b=shape[1], c=shape[2])
        return a

    def bf16(self, *shape):
        n = int(np.prod(shape))
        assert n % 2 == 0
        a = self.t[:, self.off:self.off + n // 2].bitcast(BF16)
        self.off += n // 2
        assert self.off <= self.n, ("arena overflow", self.off, self.n)
        if len(shape) == 2:
            a = a.rearrange("p (a b) -> p a b", b=shape[1])
        elif len(shape) == 3:
            a = a.rearrange("p (a b c) -> p a b c", b=shape[1], c=shape[2])
        elif len(shape) == 4:
            a = a.rearrange("p (a b c d) -> p a b c d", b=shape[1], c=shape[2], d=shape[3])
        return a


def build(cfg=None):
    cfg = cfg or {}
    dump = cfg.get('dump', ())
    stop_after = cfg.get('stop_after', 99)
    WINS = cfg.get('wins', (0, 1, 2))
    nc = bass.Bass("TRN2", target_bir_lowering=False)
    P = Prog()
    stack = ExitStack()

    def din(n, s, dt=F32):
        return nc.dram_tensor(n, list(s), dt, kind="ExternalInput").ap()

    def dscr(n, s, dt=BF16):
        kind = "ExternalOutput" if n in dump else "Internal"
        return nc.dram_tensor(n, list(s), dt, kind=kind).ap()

    xp = din("xp", [2, 2048, 2048])
    xsw = din("xsw", [3072, 2048])
    xsf = din("xsf", [8192, 2048])
    ccol_d = din("ccol", [128, 48])
    w_ada = din("w_ada", [2048, 12288])
    b_ada_col_d = din("b_ada_col", [128, 96])
    b_ada_row = din("b_ada_row", [1, 12288])
    gcols_d = din("gcols", [128, 48])
    grows = din("grows", [2, 2048])
    w_in = din("w_in", [2048, 4096])
    w_four = din("w_four", [4, 256, 256])
    w_out = din("w_out", [2048, 2048])
    w_mi = din("w_mi", [2048, 8192])
    w_mo = din("w_mo", [8192, 2048])
    tabc_d = din("tabc", [128, 1024], BF16)
    tabp = din("tabp", [4, 128, 2 * 16 * 512], BF16)
    tabs = din("tabs", [2, 8, 128, 2 * 8 * 512], BF16)
    dmask_d = din("dmask", [128, 256])
    edge_d = din("edge", [128, 8])
    identf_d = din("identf", [128, 128])
    identb_d = din("identb", [128, 128], BF16)

    yp = nc.dram_tensor("yp", [2, 2048, 2048], F32, kind="ExternalOutput").ap()
    ys = nc.dram_tensor("ys", [1024, 2048], F32, kind="ExternalOutput").ap()

    Wb_in_n = dscr("Wb_in", [2048, 4096])
    Wb_in = Wb_in_n.rearrange("(dc p) (pc n) -> pc p dc n", p=128, n=512)
    Wb_out = dscr("Wb_out", [4, 128, 16, 512])
    Wb_mi_n = dscr("Wb_mi", [2048, 8192])
    Wb_mi = Wb_mi_n.rearrange("(dc p) (fg n) -> fg p dc n", p=128, n=512)
    Wb_mo_n = dscr("Wb_mo", [8192, 2048])
    Wb_mo = Wb_mo_n.rearrange("(pcs f p) (nb n) -> nb pcs p f n", p=128, f=16, n=512)
    GP = dscr("GP", [3, 2, 2048], F32)
    SOWN = [2048, 2048, 1024]
    KLEN = [2048, 2048, 3072]
    QT = [dscr("QT%d" % w, [8, 128, SOWN[w]]) for w in range(3)]
    KT = [dscr("KT%d" % w, [8, 128, KLEN[w]]) for w in range(3)]
    VV = [dscr("VV%d" % w, [4096, 1024]) for w in range(3)]
    PQ = [dscr("PQ%d" % w, [[2048, 2048, 8192][w], 2048]) for w in range(3)]
    AT = [dscr("AT%d" % w, [1024, SOWN[w]]) for w in range(3)]
    FT = [dscr("FT%d" % w, [1024, SOWN[w]]) for w in range(3)]

    def sb(name, shape, dt):
        return stack.enter_context(nc.sbuf_tensor(name, list(shape), dt))

    identb = sb("identb_s", [128, 128], BF16)
    identf = sb("identf_s", [128, 128], F32)
    onesb = sb("onesb", [128, 128], BF16)
    dm = sb("dm", [128, 256], F32)
    edge = sb("edge_s", [128, 8], F32)
    cols_t = sb("cols", [128, 4 * 16 * 3], F32)
    cols = cols_t[:, :].rearrange("p (a b c) -> p a b c", a=4, b=16)
    gcols_t = sb("gcols_s", [128, 48], F32)
    gcols = gcols_t[:, :].rearrange("p (a b) -> p a b", b=16)
    AB_t = sb("AB", [128, 4 * 2 * 512], BF16)
    AB = AB_t[:, :].rearrange("p (g c n) -> p g c n", g=4, c=2)
    modcol_t = sb("modcol", [128, 288], F32)
    modcol = modcol_t[:, :].rearrange("p (a b c) -> p a b c", a=6, b=16)
    bcol_t = sb("bcol", [128, 96], F32)
    bcol = bcol_t[:, :].rearrange("p (a b) -> p a b", b=16)
    scT_t = sb("scT", [128, 48], BF16)
    scT = scT_t[:, :].rearrange("p (a b) -> p a b", b=3)
    ssqa = sb("ssqa", [128, 40], F32)
    epsc = sb("epsc", [128, 1], F32)
    ssqf = sb("ssqf", [128, 40], F32)
    rsa = sb("rsa", [128, 40], F32)
    rsf = sb("rsf", [128, 40], F32)
    NA = 48800
    arena_t = sb("arena", [128, NA], F32)
    ar = Arena(arena_t, NA)
    ps = [stack.enter_context(nc.psum_tensor("ps%d" % i, [128, 512], F32)) for i in range(8)]
    psb = [p[:, 0:256].bitcast(BF16) for p in ps]

    evac_ctr = [0]

    def evac_copy(out, in_, r, w):
        evac_ctr[0] += 1
        if evac_ctr[0] % 2 == 0:
            P.add('act', lambda e: e.activation(out=out, in_=in_, func=AF.Copy), r=r, w=w)
        else:
            P.add('dve', lambda e: e.tensor_copy(out=out, in_=in_), r=r, w=w)

    def mm_group(out, pairs, r, w):
        def fn(pe):
            n = len(pairs)
            ins = None
            for i, (l, rr) in enumerate(pairs):
                ins = pe.matmul(out, l, rr, start=(i == 0), stop=(i == n - 1))
            return ins
        P.add('pe', fn, r=r, w=w)

    def load(out, in_, w, r=(), eng='sp', nobar=False):
        P.add(eng, lambda e: e.dma_start(out=out, in_=in_), r=r, w=w, dma=True, nobar=nobar)

    def store(out, in_, r, w=(), eng='act'):
        P.add(eng, lambda e: e.dma_start(out=out, in_=in_), r=r, w=w, dma=True)

    load(identb[:, :], identb_d[:, :], w=['identb'])
    load(identf[:, :], identf_d[:, :], w=['identf'])
    load(dm[:, :], dmask_d[:, :], w=['dm'])
    load(edge[:, :], edge_d[:, :], w=['edge'])
    load(gcols_t[:, :], gcols_d[:, :], w=['gcols'])
    P.add('dve', lambda e: e.memset(onesb[:, :], 1.0), w=['onesb'])
    P.add('dve', lambda e: e.memset(epsc[:, :], EPS), w=['epsc'])
    P.add('dve', lambda e: e.memset(ssqa[:, :], 1.0), w=['ssqa'])
    P.add('dve', lambda e: e.memset(ssqf[:, :], 1.0), w=['ssqf'])

    for pc in range(8):
        load(Wb_in_n[pc * 256:(pc + 1) * 256, :], w_in[pc * 256:(pc + 1) * 256, :], w=[('Wb_in', pc)], eng='pool')

    ar.reset()
    zt = arena_t[:, NA - 1024:NA].bitcast(BF16).rearrange("p (a b) -> p a b", b=1024)
    P.add('dve', lambda e: e.memset(zt, 0.0), w=['zt'])

    cT = ar.f32(48)
    wa = [ar.bf16(16, 512) for _ in range(2)]
    wa32 = [ar.f32(16, 512) for _ in range(2)]
    load(cT, ccol_d[:, :], w=['cT'])
    P.add('dve', lambda e: e.memset(modcol, 0.0), w=['modcol'])
    load(bcol, b_ada_col_d[:, :].rearrange("p (a b) -> p a b", b=16), w=['bcol'])
    P.add('act', lambda e: e.activation(out=scT, in_=cT.rearrange("p (a b) -> p a b", b=3), func=AF.Silu),
          r=['cT'], w=['scT'])
    w_ada_v = w_ada.rearrange("(dc p) (j n) -> j p dc n", p=128, n=512)
    gstate = [0]

    def mod_block(j, wj, kw, pcol, pgate, gbufs):
        typ, jj = j // 4, j % 4
        if typ in (2, 5):
            gi = gstate[0]
            gstate[0] += 1
            gate = 0 if typ == 2 else 1
            pk = ('ps', pgate)
            pst = ps[pgate]
            mm_group(pst[0:3, :], [(scT[:, dc, :], wj[:, dc, :]) for dc in range(16)],
                     r=['scT', kw], w=[pk])
            br, gr, gt = gbufs[0][gi % 2], gbufs[1][gi % 2], gbufs[2][gi % 2]
            c0 = typ * 2048 + jj * 512
            load(br[0:3, :], b_ada_row[0:1, c0:c0 + 512].partition_broadcast(3)[:, 0, :], w=[('brow', gi % 2)])
            load(gr[0:3, :], grows[gate:gate + 1, jj * 512:(jj + 1) * 512].partition_broadcast(3)[:, 0, :],
                 w=[('grow', gi % 2)])
            P.add('dve', lambda e: e.tensor_tensor(out=gt[0:3, :], in0=pst[0:3, :], in1=br[0:3, :], op=ALU.add),
                  r=[pk, ('brow', gi % 2)], w=[('gtmp', gi % 2)])
            P.add('dve', lambda e: e.tensor_tensor(out=gt[0:3, :], in0=gt[0:3, :], in1=gr[0:3, :], op=ALU.mult),
                  r=[('grow', gi % 2)], w=[('gtmp', gi % 2), ('brow', gi % 2)])
            store(GP[:, gate, jj * 512:(jj + 1) * 512], gt[0:3, :], r=[('gtmp', gi % 2)], w=[('gtmp', gi % 2)])
        else:
            pk = ('ps', pcol)
            pst = ps[pcol]

            def fn(pe):
                ins = None
                for t in range(4):
                    for dc in range(16):
                        ins = pe.matmul(pst[:, t * 3:t * 3 + 3], wj[:, dc, t * 128:(t + 1) * 128], scT[:, dc, :],
                                        start=(dc == 0), stop=(dc == 15))
                return ins
            P.add('pe', fn, r=['scT', kw], w=[pk])
            P.add('dve', lambda e: e.tensor_copy(
                out=modcol[:, typ, jj * 4:(jj + 1) * 4, :], in_=pst[:, 0:12].rearrange("p (a b) -> p a b", b=3)),
                r=[pk], w=['modcol'])

    def mod_finish(t_sh, t_sc, k_g, k_sh, gsel, key):
        for typ in (t_sh, t_sc):
            P.add('dve', lambda e, typ=typ: e.tensor_tensor(out=modcol[:, typ], in0=modcol[:, typ],
                                                            in1=bcol[:, typ].unsqueeze(2).to_broadcast([128, 16, 3]), op=ALU.add),
                  r=['bcol'], w=['modcol'])
        P.add('dve', lambda e: e.scalar_tensor_tensor(
            out=cols[:, k_g], in0=modcol[:, t_sc], scalar=1.0,
            in1=gcols[:, gsel, :].unsqueeze(2).to_broadcast([128, 16, 3]), op0=ALU.add, op1=ALU.mult),
            r=['modcol', 'gcols'], w=[key])
        P.add('dve', lambda e: e.tensor_copy(out=cols[:, k_sh], in_=modcol[:, t_sh]), r=['modcol'], w=[key])

    for j in range(8):
        wj = wa[j % 2]
        kw = ('wa', j % 2)
        w32 = wa32[j % 2]
        k32 = ('wa32', j % 2)
        load(w32, w_ada_v[j], w=[k32])
        P.add('dve', lambda e, wj=wj, w32=w32: e.tensor_copy(out=wj[:, 0:8, :], in_=w32[:, 0:8, :]), r=[k32], w=[kw])
        P.add('act', lambda e, wj=wj, w32=w32: e.activation(out=wj[:, 8:16, :], in_=w32[:, 8:16, :], func=AF.Copy), r=[k32], w=[kw])
        mod_block(j, wj, kw, j % 2, 4, None)
    mod_finish(0, 1, 0, 1, 0, 'cols1')

    wf = ar.bf16(4, 2, 256)
    tcs = ar.bf16(2, 2, 256)
    load(wf, w_four.rearrange("g (cc p) e -> p g cc e", p=128), w=['wf'], eng='pool')
    load(tcs, tabc_d[:, :].rearrange("p (a b c) -> p a b c", a=2, b=2), w=['tcs'])
    for g in range(4):
        for cc in range(2):
            pk = ('ps', 6 + (g * 2 + cc) % 2)
            pst = ps[6 + (g * 2 + cc) % 2]

            def fn(pe, pst=pst, g=g, cc=cc):
                ins = None
                for s_ in range(2):
                    for c2 in range(2):
                        ins = pe.matmul(pst[:, s_ * 256:(s_ + 1) * 256], tcs[:, s_, c2, cc * 128:(cc + 1) * 128],
                                        wf[:, g, c2, :], start=(c2 == 0), stop=(c2 == 1))
                return ins
            P.add('pe', fn, r=['wf', 'tcs'], w=[pk])
            evac_copy(AB[:, g, cc, :], pst[:, :], r=[pk], w=['AB'])

    for fg in range(16):
        load(Wb_mi_n[fg * 128:(fg + 1) * 128, :], w_mi[fg * 128:(fg + 1) * 128, :], w=[('Wb_mi', fg)], eng='pool', nobar=True)
    for nt in range(16):
        load(Wb_mo_n[nt * 512:(nt + 1) * 512, :], w_mo[nt * 512:(nt + 1) * 512, :], w=[('Wb_mo', nt)], eng='pool', nobar=True)
    for w in WINS:
        regs = [(0, 1024), (3072, 4096)] if w < 2 else [(3072, 4096)]
        for (a, b) in regs:
            for r0 in range(a, b, 256):
                P.add('pool', lambda e, w=w, r0=r0: e.dma_start(out=VV[w][r0:r0 + 256, :].rearrange("(a p) n -> p a n", p=128), in_=zt),
                      r=['zt'], w=[('VVz', w, r0)], dma=True, nobar=True)
    P.barrier()

    if stop_after >= 1:
        ar.reset()
        xt = [ar.f32(2048) for _ in range(4)]
        xs = [ar.bf16(2048) for _ in range(4)]
        junk = ar.bf16(2048)
        hTs = [[ar.bf16(512) for _ in range(16)] for _ in range(2)]
        wb = [ar.bf16(16, 512) for _ in range(2)]
        wbu = [ar.bf16(16, 512) for _ in range(2)]
        wbu_loaded = [False]
        uT = [ar.bf16(512) for _ in range(8)]
        stq = [ar.bf16(512) for _ in range(4)]
        stv = ar.bf16(4, 1024)
        stpq = ar.bf16(4, 2048)
        st1 = ar.f32(8)
        blocks = []
        for w in WINS:
            if w < 2:
                for k in range(4):
                    blocks.append(dict(x=[xp[w, k * 512 + t * 128:k * 512 + (t + 1) * 128, :] for t in range(4)],
                                       b=w, w=w, q=k * 512, k=k * 512, v=1024 + k * 512, u=k * 512, pqw=w))
            else:
                for k in range(6):
                    blocks.append(dict(x=[xsw[k * 512 + t * 128:k * 512 + (t + 1) * 128, :] for t in range(4)],
                                       b=2, w=2, q=(k * 512 - 1024 if k in (2, 3) else None), k=k * 512, v=k * 512,
                                       u=None, pqw=2))
                for k in range(16):
                    blocks.append(dict(x=[xsf[k * 512 + t * 128:k * 512 + (t + 1) * 128, :] for t in range(4)],
                                       b=2, w=2, q=None, k=None, v=None, u=k * 512, pqw=2))
        wslot = [0]
        psr = [0]
        sq_i = [0]

        def nextps():
            i = 2 + psr[0] % 6
            psr[0] += 1
            return i

        def prep(bi_):
            blk = blocks[bi_]
            b = blk['b']
            hs = bi_ % 2
            hT = hTs[hs]
            for t in range(4):
                load(xt[t], blk['x'][t], w=[('xt', t)])
                P.add('act', lambda e, t=t: e.activation(out=junk, in_=xt[t], func=AF.Square, accum_out=st1[:, t:t + 1]),
                      r=[('xt', t)], w=['junk', ('st1', t)])
                P.add('act', lambda e, t=t: e.activation(out=st1[:, 4 + t:5 + t], in_=st1[:, t:t + 1], func=AF.Sqrt, scale=1.0 / 2048, bias=epsc[:, 0:1]),
                      r=[('st1', t)], w=[('st1b', t)])
                P.add('dve', lambda e, t=t: e.reciprocal(out=st1[:, 4 + t:5 + t], in_=st1[:, 4 + t:5 + t]),
                      r=[('st1b', t)], w=[('st1b', t)])
                P.add('dve', lambda e, t=t: e.tensor_scalar(out=xs[t], in0=xt[t], scalar1=st1[:, 4 + t:5 + t], scalar2=None,
                                                            op0=ALU.mult),
                      r=[('xt', t), ('st1b', t)], w=[('xs', t)])
            yield
            for dc in range(16):
                pi = dc % 2

                def fn(pe, dc=dc, pi=pi):
                    ins = None
                    for t in range(4):
                        ins = pe.transpose(psb[pi][:, t * 128:(t + 1) * 128], xs[t][:, dc * 128:(dc + 1) * 128], identb[:, :])
                    return ins
                P.add('pe', fn, r=[('xs', t) for t in range(4)] + ['identb'], w=[('ps', pi)])
                P.add('act', lambda e, dc=dc, pi=pi, b=b, hT=hT: e.activation(out=hT[dc], in_=psb[pi][:, 0:512], func=AF.Identity,
                                                                           bias=cols[:, 1, dc, b:b + 1], scale=cols[:, 0, dc, b:b + 1]),
                      r=[('ps', pi), 'cols1'], w=[('hT', hs, dc)])
                yield

        pgen = [None]

        def pstep():
            if pgen[0] is not None:
                try:
                    next(pgen[0])
                except StopIteration:
                    pgen[0] = None

        def pdrain():
            while pgen[0] is not None:
                pstep()

        pgen[0] = prep(0)
        pdrain()
        for bi_, blk in enumerate(blocks):
            b = blk['b']
            w = blk['w']
            hT = hTs[bi_ % 2]
            hkeys = [('hT', bi_ % 2, dc) for dc in range(16)]
            need = []
            if blk['q'] is not None:
                need += [0, 1]
            if blk['k'] is not None:
                need += [2, 3]
            if blk['v'] is not None:
                need += [4, 5]
            if blk['u'] is not None:
                need += [6, 7]
            pdrain()
            if bi_ + 1 < len(blocks):
                pgen[0] = prep(bi_ + 1)
            for pci, pc in enumerate(need):
                if need == [6, 7]:
                    wcur = wbu[pc - 6]
                    wk = ('wbu', pc - 6)
                    if not wbu_loaded[0]:
                        load(wcur, Wb_in[pc], w=[wk])
                        if pc == 7:
                            wbu_loaded[0] = True
                else:
                    sl = wslot[0] % 2
                    wslot[0] += 1
                    wcur = wb[sl]
                    wk = ('wb', sl)
                    load(wcur, Wb_in[pc], w=[wk])
                if pc < 4 or pc >= 6:
                    for t in range(4):
                        pi = nextps()
                        mm_group(ps[pi][:, :], [(wcur[:, dc, t * 128:(t + 1) * 128], hT[dc]) for dc in range(16)],
                                 r=hkeys + [wk], w=[('ps', pi)])
                        pstep()
                        if pc < 4:
                            h = (pc % 2) * 4 + t
                            si = sq_i[0] % 4
                            sq_i[0] += 1
                            evac_copy(stq[si], ps[pi][:, :], r=[('ps', pi)], w=[('stq', si)])
                            if pc < 2:
                                dst = QT[w][h, :, blk['q']:blk['q'] + 512]
                            else:
                                dst = KT[w][h, :, blk['k']:blk['k'] + 512]
                            store(dst, stq[si], r=[('stq', si)], w=[('stq', si)])
                        else:
                            ct = (pc - 6) * 4 + t
                            evac_copy(uT[ct], ps[pi][:, :], r=[('ps', pi)], w=[('uT', ct)])
                            if ct % 2 == 1:
                                g = ct // 2
                                for tt in range(4):
                                    pj = nextps()
                                    mm_group(ps[pj][:, :], [(uT[2 * g + cc][:, tt * 128:(tt + 1) * 128], AB[:, g, cc, :])
                                                            for cc in range(2)],
                                             r=[('uT', 2 * g), ('uT', 2 * g + 1), 'AB'], w=[('ps', pj)])
                                    pstep()
                                    o = stpq[:, tt, g * 512:(g + 1) * 512].rearrange("p (j q n) -> p q j n", j=2, q=2)
                                    i_ = ps[pj][:, :].rearrange("p (q j n) -> p q j n", q=2, j=2)
                                    evac_copy(o, i_, r=[('ps', pj)], w=[('stpq', tt)])
                    if pc == 7:
                        u0 = blk['u']
                        store(PQ[blk['pqw']][u0:u0 + 512, :].rearrange("(t p) n -> p t n", p=128), stpq,
                              r=[('stpq', tt) for tt in range(4)], w=[('stpq', tt) for tt in range(4)])
                else:
                    half = pc - 4
                    for tt in range(4):
                        pi = nextps()
                        mm_group(ps[pi][:, :], [(hT[dc][:, tt * 128:(tt + 1) * 128], wcur[:, dc, :]) for dc in range(16)],
                                 r=hkeys + [wk], w=[('ps', pi)])
                        pstep()
                        evac_copy(stv[:, tt, half * 512:(half + 1) * 512], ps[pi][:, :], r=[('ps', pi)], w=[('stv', tt)])
                    if pc == 5:
                        v0 = blk['v']
                        store(VV[w][v0:v0 + 512, :].rearrange("(t p) n -> p t n", p=128), stv,
                              r=[('stv', tt) for tt in range(4)], w=[('stv', tt) for tt in range(4)])
        P.barrier()

    if stop_after >= 2:
        ar.reset()
        kTb = [ar.bf16(4096) for _ in range(2)]
        qTb = [ar.bf16(2048) for _ in range(2)]
        NCH = {1: 17, 4: 20, 16: 32}
        Vh = [{d: ar.bf16(NCH[d], 128) for d in (1, 4, 16)} for _ in range(2)]
        accden = [ar.f32(2, 2048) for _ in range(2)]
        sbt = [ar.f32(2, 128) for _ in range(4)]
        pT = [ar.bf16(2, 128) for _ in range(4)]
        ast = [ar.bf16(2048) for _ in range(2)]
        sqb = [ar.bf16(2048) for _ in range(2)]
        for i in range(2):
            P.add('dve', lambda e, i=i: e.memset(kTb[i], 0.0), w=[('kT', i)])
        wa_bg = [ar.bf16(16, 512) for _ in range(2)]
        gb = ([ar.f32(512) for _ in range(2)], [ar.f32(512) for _ in range(2)], [ar.f32(512) for _ in range(2)])
        wo32 = [ar.f32(2048) for _ in range(2)]
        wo16 = [ar.bf16(2048) for _ in range(2)]
        bg_tasks = []

        def t_mod(j):
            wj = wa_bg[j % 2]
            kw = ('wabg', j % 2)
            load(wj, w_ada_v[j], w=[kw], eng='pool')
            mod_block(j, wj, kw, j % 2, 2 + j % 2, gb)

        def t_wo(kc):
            i = kc % 2
            load(wo32[i], w_out[kc * 128:(kc + 1) * 128, :], w=[('wo32', i)])
            P.add('dve', lambda e: e.tensor_scalar(out=wo16[i], in0=wo32[i], scalar1=gcols[:, 2, kc:kc + 1],
                                                   scalar2=None, op0=ALU.mult),
                  r=[('wo32', i), 'gcols'], w=[('wo16', i)])
            store(Wb_out[:, :, kc, :].rearrange("nb p n -> p nb n"), wo16[i].rearrange("p (a b) -> p a b", b=512),
                  r=[('wo16', i)], w=[('wo16', i)])
        for i in range(16):
            bg_tasks.append((t_mod, 8 + i))
            bg_tasks.append((t_wo, i))
        hb = 0
        tile_ctr = 0
        tbase = 0
        for w in WINS:
            S = SOWN[w]
            sample = (w == 2)
            ntt = S // 128
            tbase = 16 * w
            for h in range(8):
                bi = hb % 2
                hb += 1
                kT, qT, V_, AD = kTb[bi], qTb[bi], Vh[bi], accden[bi]
                vz_keys = [('VVz', w, r0) for r0 in ((list(range(0, 1024, 256)) + list(range(3072, 4096, 256))) if w < 2 else list(range(3072, 4096, 256)))]
                if sample:
                    load(kT[:, 0:3072], KT[w][h], w=[('kT', bi)])
                else:
                    load(kT[:, 1024:3072], KT[w][h], w=[('kT', bi)])
                load(qT[:, 0:S], QT[w][h], w=[('qT', bi)])
                for d in (1, 4, 16):
                    L = S // d
                    Lh = 1024 // d
                    nch = (L + 128 + 127) // 128
                    for r_ in range(d):
                        t0 = (Lh - 64) * d + r_
                        src = VV[w][t0:t0 + (nch * 128 - 1) * d + 1:d, h * 128:(h + 1) * 128].rearrange("(m k) n -> k m n", k=128)
                        load(V_[d][:, r_ * nch:(r_ + 1) * nch, :], src, w=[('V', bi, d, r_)], r=vz_keys)
                tiles = []
                for bidx, d in enumerate((1, 4, 16)):
                    L = S // d
                    nq = min(128, L)
                    for r_ in range(d):
                        for n in range(L // nq):
                            tiles.append((bidx, d, r_, n))
                LA = 3

                def stage1(tl, ti):
                    bidx, d, r_, n = tl
                    L = S // d
                    Lh = 1024 // d
                    nq = min(128, L)
                    ntile = L // nq
                    coef = -(2.0 ** -(h + 1)) * d / SC
                    pS = ps[ti]
                    kS = ('ps', ti)
                    qcols = slice(r_ + 128 * n * d, r_ + 128 * n * d + (nq - 1) * d + 1, d)
                    qap = qT[:, qcols]
                    kT_ = kT

                    def fn(pe, kT=kT_):
                        ins = None
                        for c in range(2):
                            k0 = (Lh - 64 + 128 * (n + c)) * d + r_
                            ins = pe.matmul(pS[:, c * 128:c * 128 + nq], kT[:, k0:k0 + 127 * d + 1:d], qap,
                                            start=True, stop=True)
                        return ins
                    P.add('pe', fn, r=[('kT', bi), ('qT', bi)], w=[kS])
                    sbv = sbt[ti][:, :, 0:nq]
                    P.add('dve', lambda e: e.scalar_tensor_tensor(
                        out=sbv, in0=dm[:, :].rearrange("p (c n) -> p c n", c=2)[:, :, 0:nq], scalar=coef,
                        in1=pS[:, 0:256].rearrange("p (c n) -> p c n", c=2)[:, :, 0:nq], op0=ALU.mult, op1=ALU.add),
                        r=[kS, 'dm'], w=[('sb', ti)])
                    ecol = [0, 0]
                    if n == 0:
                        ecol[0] = 3 if sample else 1
                    if n == ntile - 1:
                        if sample:
                            ecol[1] = 5 if d == 16 else 4
                        else:
                            ecol[1] = 2
                    pTv = pT[ti][:, :, 0:nq]
                    if ecol == [0, 0]:
                        P.add('act', lambda e: e.activation(out=pTv, in_=sbv, func=AF.Exp, scale=SC),
                              r=[('sb', ti)], w=[('pT', ti)])
                    else:
                        for c in range(2):
                            P.add('act', lambda e, c=c, ec=ecol[c]: e.activation(
                                out=pTv[:, c, :], in_=sbv[:, c, :], func=AF.Exp, scale=SC, bias=edge[:, ec:ec + 1]),
                                r=[('sb', ti), 'edge'], w=[('pT', ti)])

                def stage2(tl, ti):
                    bidx, d, r_, n = tl
                    L = S // d
                    nch = (L + 128 + 127) // 128
                    nq = min(128, L)
                    pO = ps[4 + ti]
                    kO = ('ps', 4 + ti)
                    pTv = pT[ti][:, :, 0:nq]
                    qcols = slice(r_ + 128 * n * d, r_ + 128 * n * d + (nq - 1) * d + 1, d)

                    Vd = V_[d]

                    def fn2(pe):
                        ins = None
                        for c in range(2):
                            ins = pe.matmul(pO[:, 0:nq], Vd[:, r_ * nch + n + c, :], pTv[:, c, :],
                                            start=(c == 0), stop=(c == 1))
                        for c in range(2):
                            ins = pe.matmul(pO[:, 128:128 + nq], onesb[:, :], pTv[:, c, :],
                                            start=(c == 0), stop=(c == 1))
                        return ins
                    P.add('pe', fn2, r=[('V', bi, d, r_), ('pT', ti), 'onesb'], w=[kO])
                    oap = AD[:, :, qcols]
                    iap = pO[:, 0:256].rearrange("p (c n) -> p c n", c=2)[:, :, 0:nq]
                    if bidx == 0:
                        P.add('act', lambda e: e.activation(out=oap, in_=iap, func=AF.Copy),
                              r=[kO], w=[('AD', bi)])
                    else:
                        P.add('dve', lambda e: e.tensor_tensor(out=oap, in0=iap, in1=oap, op=ALU.add),
                              r=[kO], w=[('AD', bi)])

                slots = []
                for s_ in range(len(tiles) + LA):
                    if s_ < len(tiles):
                        ti = tile_ctr % 4
                        tile_ctr += 1
                        slots.append(ti)
                        stage1(tiles[s_], ti)
                    if s_ - LA >= 0:
                        stage2(tiles[s_ - LA], slots[s_ - LA])
                P.add('act', lambda e, AD=AD, S=S: e.activation(out=AD[:, 1, 0:S], in_=AD[:, 1, 0:S], func=AF.Ln), r=[], w=[('AD', bi)])
                P.add('act', lambda e, AD=AD, S=S: e.activation(out=AD[:, 1, 0:S], in_=AD[:, 1, 0:S], func=AF.Exp, scale=-1.0), r=[], w=[('AD', bi)])
                P.add('dve', lambda e, AD=AD, S=S: e.tensor_tensor(out=AD[:, 0, 0:S], in0=AD[:, 0, 0:S], in1=AD[:, 1, 0:S], op=ALU.mult),
                      r=[], w=[('AD', bi)])
                P.add('act', lambda e, AD=AD, S=S, bi=bi: e.activation(out=ast[bi][:, 0:S], in_=AD[:, 0, 0:S], func=AF.Copy),
                      r=[('AD', bi)], w=[('ast', bi)])
                P.add('dve', lambda e, AD=AD, S=S, bi=bi: e.tensor_tensor(out=sqb[bi][:, 0:S], in0=AD[:, 0, 0:S], in1=AD[:, 0, 0:S], op=ALU.mult),
                      r=[('AD', bi)], w=[('sqb', bi)])
                store(AT[w][h * 128:(h + 1) * 128, :], ast[bi][:, 0:S], r=[('ast', bi)], w=[('ast', bi)])
                for _ in range(2):
                    if bg_tasks:
                        f_, a_ = bg_tasks.pop(0)
                        f_(a_)

                def fn3(pe, bi=bi, ntt=ntt):
                    ins = None
                    for tt in range(ntt):
                        ins = pe.matmul(ps[4][:, tt:tt + 1], sqb[bi][:, tt * 128:(tt + 1) * 128], onesb[:, 0:1],
                                        start=True, stop=True)
                    return ins
                P.add('pe', fn3, r=[('sqb', bi), 'onesb'], w=[('ps', 4)])
                if h == 0:
                    P.add('dve', lambda e, ntt=ntt, tbase=tbase: e.tensor_copy(out=ssqa[:, tbase:tbase + ntt], in_=ps[4][:, 0:ntt]),
                          r=[('ps', 4)], w=['ssqa'])
                else:
                    P.add('dve', lambda e, ntt=ntt, tbase=tbase: e.tensor_tensor(out=ssqa[:, tbase:tbase + ntt], in0=ps[4][:, 0:ntt],
                                                                                   in1=ssqa[:, tbase:tbase + ntt], op=ALU.add),
                          r=[('ps', 4)], w=['ssqa'])
            tbase += ntt
        while bg_tasks:
            f_, a_ = bg_tasks.pop(0)
            f_(a_)
        mod_finish(3, 4, 2, 3, 1, 'cols2')
        P.barrier()

    if stop_after >= 3:
        ar.reset()
        fst = [ar.bf16(512) for _ in range(8)]
        fsq = [ar.bf16(512) for _ in range(8)]
        base = ar.off
        tbase = 0
        pass_ctr = 0
        slc = 0
        for w in WINS:
            S = SOWN[w]
            ntt = S // 128
            nj = S // 512
            tbase = 16 * w
            ar.off = base
            if w < 2:
                pqr = ar.bf16(16, 2048)
                tb = [ar.bf16(2, 16, 512) for _ in range(2)]
                for ch in range(16):
                    load(pqr[:, ch, :], PQ[w][ch * 128:(ch + 1) * 128, :], w=[('pqr', ch)])
                pq5 = pqr.rearrange("p c (e q n) -> p c e q n", e=8, q=2)
            else:
                P.barrier()
                pqs = [ar.bf16(8, 1024) for _ in range(2)]
                tbs = [ar.bf16(2, 8, 512) for _ in range(2)]
            for j in range(nj):
                if w < 2:
                    tj = tb[j % 2]
                    tk = ('tb', j % 2)
                    tpv = tabp[j].rearrange("p (a b c) -> p a b c", a=2, b=16)
                    for cs_ in range(2):
                        for hh in range(2):
                            load(tj[:, cs_, hh * 8:(hh + 1) * 8, :], tpv[:, cs_, hh * 8:(hh + 1) * 8, :], w=[tk])
                for half in range(2):
                    bset = (pass_ctr % 2) * 4
                    pass_ctr += 1
                    if w < 2:
                        for e4 in range(4):
                            et = half * 4 + e4
                            pi = bset + e4
                            pairs = []
                            for ch in range(16):
                                pairs.append((pq5[:, ch, et, 0, :], tj[:, 0, ch, :]))
                                pairs.append((pq5[:, ch, et, 1, :], tj[:, 1, ch, :]))
                            mm_group(ps[pi][:, :], pairs, r=[('pqr', ch) for ch in range(16)] + [tk], w=[('ps', pi)])
                    else:
                        for grp in range(8):
                            i = slc % 2
                            slc += 1
                            src_ = PQ[2][grp * 1024:(grp + 1) * 1024, half * 1024:(half + 1) * 1024].rearrange("(c p) n -> p c n", p=128)
                            load(pqs[i], src_, w=[('pqs', i)])
                            load(tbs[i], tabs[j, grp].rearrange("p (a b c) -> p a b c", a=2, b=8), w=[('tbs', i)])
                            pv = pqs[i].rearrange("p c (e q n) -> p c e q n", e=4, q=2)
                            for e4 in range(4):
                                pi = bset + e4

                                def fn(pe, pv=pv, tt_=tbs[i], e4=e4, pi=pi, grp=grp):
                                    ins = None
                                    for ch in range(8):
                                        for q_ in range(2):
                                            ins = pe.matmul(ps[pi][:, :], pv[:, ch, e4, q_, :], tt_[:, q_, ch, :],
                                                            start=(grp == 0 and ch == 0 and q_ == 0),
                                                            stop=(grp == 7 and ch == 7 and q_ == 1))
                                    return ins
                                P.add('pe', fn, r=[('pqs', i), ('tbs', i)], w=[('ps', pi)])
                    for e4 in range(4):
                        et = half * 4 + e4
                        pi = bset + e4
                        P.add('dve', lambda e, et=et, pi=pi: e.tensor_copy(out=fst[et], in_=ps[pi][:, :]),
                              r=[('ps', pi)], w=[('fst', et)])
                        P.add('act', lambda e, et=et, pi=pi: e.activation(out=fsq[et], in_=fst[et], func=AF.Square),
                              r=[('fst', et)], w=[('fsq', et)])
                        store(FT[w][et * 128:(et + 1) * 128, j * 512:(j + 1) * 512], fst[et], r=[('fst', et)], w=[('fst', et)])
                pq_ = bset

                def fnq(pe, pq_=pq_):
                    ins = None
                    for tt in range(4):
                        for et in range(8):
                            ins = pe.matmul(ps[pq_][:, tt:tt + 1], fsq[et][:, tt * 128:(tt + 1) * 128], onesb[:, 0:1],
                                            start=(et == 0), stop=(et == 7))
                    return ins
                P.add('pe', fnq, r=[('fsq', et) for et in range(8)] + ['onesb'], w=[('ps', pq_)])
                c0 = tbase + j * 4
                P.add('dve', lambda e, pq_=pq_, c0=c0: e.tensor_copy(out=ssqf[:, c0:c0 + 4], in_=ps[pq_][:, 0:4]),
                      r=[('ps', pq_)], w=['ssqf'])
            tbase += ntt
        P.barrier()

    if stop_after >= 4:
        ar.reset()
        NT = 40
        for (src_, dst_, nm) in ((ssqa, rsa, 'rsa'), (ssqf, rsf, 'rsf')):
            P.add('act', lambda e, src_=src_, dst_=dst_: e.activation(out=dst_[:, 0:NT], in_=src_[:, 0:NT], func=AF.Sqrt, scale=1.0 / 1024, bias=epsc[:, 0:1]), w=[nm])
            P.add('dve', lambda e, dst_=dst_: e.reciprocal(out=dst_[:, 0:NT], in_=dst_[:, 0:NT]), r=[nm], w=[nm])
        xt = [ar.f32(2048) for _ in range(4)]
        r1o = ar.off
        aT = ar.bf16(8, 512)
        fT = ar.bf16(8, 512)
        gp1 = ar.f32(2048)
        y1 = [ar.f32(2048) for _ in range(4)]
        tmpA = [ar.f32(512) for _ in range(2)]
        ar.off = r1o
        hid = [ar.bf16(512) for _ in range(64)]

        def R1(a, n):
            return [('R1', i) for i in range(a, a + n)]
        k_aT, k_fT, k_gp1 = R1(0, 8), R1(8, 8), R1(16, 8)
        k_y1 = [R1(24 + 8 * t, 8) for t in range(4)]
        k_tmpA = [R1(56, 2), R1(58, 2)]
        y2 = [ar.f32(2048) for _ in range(4)]
        xs2 = [y2[t][:, 0:1024].bitcast(BF16) for t in range(4)]
        WS = [ar.bf16(8192) for _ in range(2)]
        h2o = ar.off
        h2T = [ar.bf16(512) for _ in range(16)]
        ar.off = h2o
        y2T = [ar.f32(512) for _ in range(4)]
        ar.off = h2o + 4096
        gp2 = ar.f32(2048)
        rtmp = [ar.f32(512) for _ in range(2)]
        st3 = ar.f32(16)
        wsc = [0]
        psr3 = [0]
        rl = [0]
        blocks3 = []
        tb_ = {0: 0, 1: 16, 2: 32}
        for w in WINS:
            for k in range(SOWN[w] // 512):
                blocks3.append((w, k * 512))

        def ws_load(src_ap, shape3, extra_r=()):
            i = wsc[0] % 2
            wsc[0] += 1
            v = WS[i].rearrange("p (a b) -> p a b", b=shape3[1])
            load(v, src_ap, w=[('WS', i)], r=list(extra_r))
            return v, ('WS', i)

        for (w, tok0) in blocks3:
            b = w
            tile0 = tb_[w] + tok0 // 128
            load(aT, AT[w][:, tok0:tok0 + 512].rearrange("(kc p) n -> p kc n", p=128), w=k_aT)
            load(fT, FT[w][:, tok0:tok0 + 512].rearrange("(kc p) n -> p kc n", p=128), w=k_fT)
            pre_w = ws_load(Wb_out[0], (16, 512))
            load(gp1, GP[b, 0:1, :].partition_broadcast(128)[:, 0, :], w=k_gp1)
            for t in range(4):
                if w < 2:
                    xsrc = xp[w, tok0 + t * 128:tok0 + (t + 1) * 128, :]
                else:
                    xsrc = xsw[1024 + tok0 + t * 128:1024 + tok0 + (t + 1) * 128, :]
                load(xt[t], xsrc, w=[('xt', t)])
            load(gp2, GP[b, 1:2, :].partition_broadcast(128)[:, 0, :], w=['gp2'])
            for nb in range(4):
                wv, wk = pre_w if nb == 0 else ws_load(Wb_out[nb], (16, 512))
                for tt in range(4):
                    pa = psr3[0] % 8
                    pb = (psr3[0] + 1) % 8
                    psr3[0] += 2
                    mm_group(ps[pa][:, :], [(aT[:, kc, tt * 128:(tt + 1) * 128], wv[:, kc, :]) for kc in range(8)],
                             r=k_aT + [wk], w=[('ps', pa)])
                    mm_group(ps[pb][:, :], [(fT[:, kc, tt * 128:(tt + 1) * 128], wv[:, 8 + kc, :]) for kc in range(8)],
                             r=k_fT + [wk], w=[('ps', pb)])
                    ti = (nb * 4 + tt) % 2
                    tl = tile0 + tt
                    P.add('act', lambda e, ti=ti, pa=pa, tl=tl: e.activation(out=tmpA[ti], in_=ps[pa][:, :], func=AF.Copy,
                                                                            scale=rsa[:, tl:tl + 1]),
                          r=[('ps', pa), 'rsa'], w=k_tmpA[ti])
                    P.add('dve', lambda e, ti=ti, pb=pb, tl=tl, tt=tt, nb=nb: e.scalar_tensor_tensor(
                        out=y1[tt][:, nb * 512:(nb + 1) * 512], in0=ps[pb][:, :], scalar=rsf[:, tl:tl + 1], in1=tmpA[ti],
                        op0=ALU.mult, op1=ALU.add),
                        r=[('ps', pb), 'rsf'] + k_tmpA[ti], w=k_y1[tt])
            for t in range(4):
                P.add('act', lambda e, t=t: e.activation(out=y2[t], in_=y1[t], func=AF.Square, accum_out=st3[:, t:t + 1]),
                      r=k_y1[t], w=[('y2', t), ('st3', t)])
                P.add('act', lambda e, t=t: e.activation(out=st3[:, t:t + 1], in_=st3[:, t:t + 1], func=AF.Sqrt, scale=1.0 / 2048, bias=epsc[:, 0:1]), r=[], w=[('st3', t)])
                P.add('dve', lambda e, t=t: e.reciprocal(out=st3[:, t:t + 1], in_=st3[:, t:t + 1]), r=[], w=[('st3', t)])
                P.add('dve', lambda e, t=t: e.scalar_tensor_tensor(out=y1[t], in0=y1[t], scalar=st3[:, t:t + 1], in1=gp1,
                                                                   op0=ALU.mult, op1=ALU.mult),
                      r=k_gp1 + [('st3', t)], w=k_y1[t])
                P.add('pool' if t % 2 == 0 else 'dve', lambda e, t=t: e.tensor_tensor(out=xt[t], in0=y1[t], in1=xt[t], op=ALU.add),
                      r=k_y1[t], w=[('xt', t)])
            for t in range(4):
                P.add('act', lambda e, t=t: e.activation(out=y2[t], in_=xt[t], func=AF.Square, accum_out=st3[:, 4 + t:5 + t]),
                      r=[('xt', t)], w=[('y2', t), ('st3b', t)])
                P.add('act', lambda e, t=t: e.activation(out=st3[:, 4 + t:5 + t], in_=st3[:, 4 + t:5 + t], func=AF.Sqrt, scale=1.0 / 2048, bias=epsc[:, 0:1]), r=[], w=[('st3b', t)])
                P.add('dve', lambda e, t=t: e.reciprocal(out=st3[:, 4 + t:5 + t], in_=st3[:, 4 + t:5 + t]), r=[], w=[('st3b', t)])
                P.add('dve', lambda e, t=t: e.tensor_scalar(out=xs2[t], in0=xt[t], scalar1=st3[:, 4 + t:5 + t], scalar2=None,
                                                            op0=ALU.mult),
                      r=[('xt', t), ('st3b', t)], w=[('y2', t)])
            for dc in range(16):
                pi = dc % 4

                def fn(pe, dc=dc, pi=pi):
                    ins = None
                    for t in range(4):
                        ins = pe.transpose(psb[pi][:, t * 128:(t + 1) * 128], xs2[t][:, dc * 128:(dc + 1) * 128], identb[:, :])
                    return ins
                P.add('pe', fn, r=[('y2', t) for t in range(4)] + ['identb'], w=[('ps', pi)])
                P.add('act', lambda e, dc=dc, pi=pi, b=b: e.activation(out=h2T[dc], in_=psb[pi][:, 0:512], func=AF.Identity,
                                                                     bias=cols[:, 3, dc, b:b + 1], scale=cols[:, 2, dc, b:b + 1]),
                      r=[('ps', pi), 'cols2'], w=[('h2T', dc)])
            h2keys = [('h2T', dc) for dc in range(16)]
            for fg in range(16):
                wv, wk = ws_load(Wb_mi[fg], (16, 512), extra_r=[('Wb_mi', q_) for q_ in range(16)])
                for t in range(4):
                    ft = fg * 4 + t
                    pi = 4 + psr3[0] % 4
                    psr3[0] += 1
                    mm_group(ps[pi][:, :], [(wv[:, dc, t * 128:(t + 1) * 128], h2T[dc]) for dc in range(16)],
                             r=h2keys + [wk], w=[('ps', pi)])
                    ri = rl[0] % 2
                    rl[0] += 1
                    P.add('act', lambda e, ri=ri, pi=pi: e.activation(out=rtmp[ri], in_=ps[pi][:, :], func=AF.Relu),
                          r=[('ps', pi)], w=[('rtmp', ri)])
                    if ft % 2 == 0:
                        P.add('dve', lambda e, ri=ri, ft=ft: e.tensor_tensor(out=hid[ft], in0=rtmp[ri], in1=rtmp[ri], op=ALU.mult),
                              r=[('rtmp', ri)], w=R1(ft, 1))
                    else:
                        P.add('pool', lambda e, ri=ri, ft=ft: e.tensor_tensor(out=hid[ft], in0=rtmp[ri], in1=rtmp[ri], op=ALU.mult),
                              r=[('rtmp', ri)], w=R1(ft, 1))
            hkeys = R1(0, 64)
            mo_keys = [('Wb_mo', q_) for q_ in range(16)]
            for nb in range(4):
                bb = 4 * (nb % 2)
                for pcs in range(4):
                    wv, wk = ws_load(Wb_mo[nb, pcs], (16, 512), extra_r=mo_keys)
                    for tt in range(4):
                        def fnm(pe, wv=wv, tt=tt, pcs=pcs, bb=bb):
                            ins = None
                            for f16 in range(16):
                                ins = pe.matmul(ps[bb + tt][:, :], hid[pcs * 16 + f16][:, tt * 128:(tt + 1) * 128], wv[:, f16, :],
                                                start=(pcs == 0 and f16 == 0), stop=(pcs == 3 and f16 == 15))
                            return ins
                        P.add('pe', fnm, r=R1(pcs * 16, 16) + [wk], w=[('ps', bb + tt)])
                for tt in range(4):
                    evac_copy(y2[tt][:, nb * 512:(nb + 1) * 512], ps[bb + tt][:, :], r=[('ps', bb + tt)], w=[('y2', tt)])
            for t in range(4):
                P.add('act', lambda e, t=t: e.activation(out=y1[t], in_=y2[t], func=AF.Square, accum_out=st3[:, 8 + t:9 + t]),
                      r=[('y2', t)], w=k_y1[t] + [('st3c', t)])
                P.add('act', lambda e, t=t: e.activation(out=st3[:, 8 + t:9 + t], in_=st3[:, 8 + t:9 + t], func=AF.Sqrt, scale=1.0 / 2048, bias=epsc[:, 0:1]), r=[], w=[('st3c', t)])
                P.add('dve', lambda e, t=t: e.reciprocal(out=st3[:, 8 + t:9 + t], in_=st3[:, 8 + t:9 + t]), r=[], w=[('st3c', t)])
                P.add('dve', lambda e, t=t: e.scalar_tensor_tensor(out=y2[t], in0=y2[t], scalar=st3[:, 8 + t:9 + t], in1=gp2,
                                                                   op0=ALU.mult, op1=ALU.mult),
                      r=['gp2', ('st3c', t)], w=[('y2', t)])
                P.add('pool' if t % 2 == 0 else 'dve', lambda e, t=t: e.tensor_tensor(out=y2[t], in0=y2[t], in1=xt[t], op=ALU.add),
                      r=[('xt', t)], w=[('y2', t)])
                if w < 2:
                    dst = yp[w, tok0 + t * 128:tok0 + (t + 1) * 128, :]
                else:
                    dst = ys[tok0 + t * 128:tok0 + (t + 1) * 128, :]
                store(dst, y2[t], r=[('y2', t)], w=[('y2', t)], eng=('pool' if t % 2 == 0 else 'act'))
    return nc, P, stack


_CONST = {}


def _bf(a):
    return np.ascontiguousarray(a.astype(np.float32)).astype(ml_dtypes.bfloat16)


def _consts():
    if _CONST:
        return _CONST
    C = _CONST
    C['identf'] = np.eye(128, dtype=np.float32)
    C['identb'] = _bf(np.eye(128))
    kk = np.arange(128)[:, None]
    ii = np.arange(128)[None, :]
    d0 = np.where(kk >= ii, np.abs(ii + 64 - kk), DBIG).astype(np.float32)
    d1 = np.where(kk <= ii, np.abs(ii - 64 - kk), DBIG).astype(np.float32)
    C['dmask'] = np.ascontiguousarray(np.concatenate([d0, d1], axis=1))
    cp = (np.arange(2)[None, :, None] * 128 + np.arange(128)[:, None, None])
    c = np.arange(256)[None, None, :]
    ang = 2 * np.pi * ((cp * c) % 256) / 256.0
    tc = np.stack([np.cos(ang) / 16.0, -np.sin(ang) / 16.0], axis=1)
    C['tabc'] = _bf(tc.reshape(128, 1024))
    S = 2048
    lut_c = np.cos(2 * np.pi * np.arange(S) / S) / np.sqrt(S)
    lut_s = np.sin(2 * np.pi * np.arange(S) / S) / np.sqrt(S)
    s = (np.arange(16)[None, :, None] * 128 + np.arange(128)[:, None, None])
    tabp = np.empty((4, 128, 2, 16, 512), dtype=np.float32)
    for j in range(4):
        sp_ = j * 512 + np.arange(512)[None, None, :]
        k = (s * sp_) % S
        tabp[j, :, 0] = lut_c[k]
        tabp[j, :, 1] = lut_s[k]
    C['tabp'] = _bf(tabp.reshape(4, 128, 2 * 16 * 512))
    S = 8192
    lut_c = (np.cos(2 * np.pi * np.arange(S) / S) / np.sqrt(S)).astype(np.float32)
    lut_s = (np.sin(2 * np.pi * np.arange(S) / S) / np.sqrt(S)).astype(np.float32)
    tabs_all = []
    for core in range(8):
        t = np.empty((2, 8, 128, 2, 8, 512), dtype=np.float32)
        for j in range(2):
            sp_ = 1024 * core + j * 512 + np.arange(512, dtype=np.int64)[None, None, :]
            for grp in range(8):
                s = ((grp * 8 + np.arange(8, dtype=np.int64))[None, :, None] * 128 + np.arange(128, dtype=np.int64)[:, None, None])
                k = (s * sp_) % S
                t[j, grp, :, 0] = lut_c[k]
                t[j, grp, :, 1] = lut_s[k]
        tabs_all.append(_bf(t.reshape(2, 8, 128, 2 * 8 * 512)))
    C['tabs'] = tabs_all
    edges = []
    for core in range(8):
        vl = NEG if core == 0 else 0.0
        vr = NEG if core == 7 else 0.0
        e = np.zeros((128, 8), dtype=np.float32)
        e[:64, 1] = NEG
        e[64:, 2] = NEG
        e[:64, 3] = vl
        e[64:, 4] = vr
        e[:64, 5] = vr
        e[64:, 5] = NEG
        edges.append(e)
    C['edge'] = edges
    return C


def make_in_maps(inputs, n_cores=8):
    C = _consts()
    f32 = lambda a: np.ascontiguousarray(np.asarray(a, dtype=np.float32))
    x_prompt = np.asarray(inputs['x_prompt'], dtype=np.float32)
    x_sample = f32(np.asarray(inputs['x_sample'])[0])
    c_prompt = np.asarray(inputs['c_prompt'], dtype=np.float32)
    c_sample = np.asarray(inputs['c_sample'], dtype=np.float32)
    w_ada = f32(np.asarray(inputs['w_ada'])[0])
    b_ada = np.asarray(inputs['b_ada'], dtype=np.float32)[0]
    shared = {
        'xsf': x_sample,
        'w_ada': w_ada,
        'b_ada_col': f32(b_ada.reshape(96, 128).T),
        'b_ada_row': f32(b_ada.reshape(1, 12288)),
        'w_in': f32(np.asarray(inputs['w_in'])[0]),
        'w_four': f32(np.asarray(inputs['w_fourier'])[0]),
        'w_out': f32(np.asarray(inputs['w_out'])[0]),
        'w_mi': f32(np.asarray(inputs['w_mlp_in'])[0]),
        'w_mo': f32(np.asarray(inputs['w_mlp_out'])[0]),
        'tabc': C['tabc'], 'tabp': C['tabp'], 'dmask': C['dmask'],
        'identf': C['identf'], 'identb': C['identb'],
    }
    gout = np.concatenate([np.asarray(inputs['g_attn_out'], dtype=np.float32)[0],
                           np.asarray(inputs['g_fourier_out'], dtype=np.float32)[0]])
    gc = np.stack([np.asarray(inputs['g_pre_mix'], dtype=np.float32)[0].reshape(16, 128).T,
                   np.asarray(inputs['g_pre_mlp'], dtype=np.float32)[0].reshape(16, 128).T,
                   gout.reshape(16, 128).T], axis=1)
    shared['gcols'] = f32(gc.reshape(128, 48))
    shared['grows'] = f32(np.stack([np.asarray(inputs['g_post_mix'], dtype=np.float32)[0],
                                    np.asarray(inputs['g_post_mlp'], dtype=np.float32)[0]]))
    maps = []
    for core in range(n_cores):
        m = dict(shared)
        m['xp'] = np.ascontiguousarray(x_prompt[2 * core:2 * core + 2])
        xw = np.zeros((3072, 2048), dtype=np.float32)
        lo, hi = 1024 * (core - 1), 1024 * (core + 2)
        a, b = max(lo, 0), min(hi, 8192)
        xw[a - lo:b - lo] = x_sample[a:b]
        m['xsw'] = xw
        cs = np.stack([c_prompt[2 * core], c_prompt[2 * core + 1], c_sample[0]], axis=1)
        m['ccol'] = f32(cs.reshape(16, 128, 3).transpose(1, 0, 2).reshape(128, 48))
        m['tabs'] = C['tabs'][core]
        m['edge'] = C['edge'][core]
        maps.append(m)
    return maps


_NC = {}


def get_nc(cfg=None):
    key = repr(cfg)
    if key not in _NC:
        nc, P, stack = build(cfg)
        P.emit(nc, stack)
        stack.close()
        _NC[key] = nc
    return _NC[key]


def kernel(**inputs):
    maps = make_in_maps(inputs)
    nc = get_nc(None)
    res = run_bass_kernel_spmd(nc, maps, core_ids=list(range(8)))
    y_prompt = np.empty((16, 2048, 2048), dtype=np.float32)
    y_sample = np.empty((1, 8192, 2048), dtype=np.float32)
    for core in range(8):
        r = res.results[core]
        y_prompt[2 * core:2 * core + 2] = np.asarray(r['yp'], dtype=np.float32)
        y_sample[0, 1024 * core:1024 * (core + 1)] = np.asarray(r['ys'], dtype=np.float32)
    return (y_prompt, y_sample)
```

```python
import numpy as np
import ml_dtypes
from contextlib import ExitStack
import concourse.bass as bass
import concourse.mybir as mybir
from concourse.bass_utils import run_bass_kernel_spmd

F32 = mybir.dt.float32
BF16 = mybir.dt.bfloat16
AF = mybir.ActivationFunctionType
ALU = mybir.AluOpType
AX = mybir.AxisListType

NEG = -30000.0
EPS = 1e-6
SC = float(128 ** -0.5)
DBIG = 1.0e9
ENGS = ('pe', 'act', 'dve', 'pool', 'sp')
NDS = 16


class Op:
    __slots__ = ('eng', 'fn', 'deps', 'dma', 'ms', 'dj', 'sig')

    def __init__(self, eng, fn, deps, dma):
        self.eng = eng
        self.fn = fn
        self.deps = deps
        self.dma = dma
        self.ms = 0
        self.dj = -1
        self.sig = False


class Prog:
    def __init__(self):
        self.ops = []
        self.lw = {}
        self.rd = {}
        self.bar = set()
        self.bar_pending = set()
        self.last_on = {}
        self.all_dma = []

    def add(self, eng, fn, r=(), w=(), dma=False, nobar=False):
        deps = set()
        for k in r:
            i = self.lw.get(k)
            if i is not None:
                deps.add(i)
        for k in w:
            i = self.lw.get(k)
            if i is not None:
                deps.add(i)
            deps.update(self.rd.get(k, ()))
        if eng in self.bar_pending:
            deps.update(self.bar)
            self.bar_pending.discard(eng)
        idx = len(self.ops)
        self.ops.append(Op(eng, fn, deps, dma))
        for k in r:
            self.rd.setdefault(k, []).append(idx)
        for k in w:
            self.lw[k] = idx
            self.rd[k] = []
        if not nobar:
            self.last_on[eng] = idx
        if dma and not nobar:
            self.all_dma.append(idx)
        return idx

    def barrier(self):
        self.bar = set(self.last_on.values()) | set(self.all_dma)
        self.bar_pending = set(ENGS)

    def emit(self, nc, stack):
        ops = self.ops
        for op in ops:
            for d in op.deps:
                ops[d].sig = True
        cnt = {e: 0 for e in ENGS}
        dcnt = {e: 0 for e in ENGS}
        dlist = {e: [] for e in ENGS}
        for i, op in enumerate(ops):
            if op.dma:
                op.dj = dcnt[op.eng]
                dcnt[op.eng] += 1
                dlist[op.eng].append(i)
            elif op.sig:
                cnt[op.eng] += 1
                op.ms = cnt[op.eng]
        for e in ENGS:
            assert cnt[e] < 60000, (e, cnt[e])
        sem = {e: stack.enter_context(nc.semaphore("s_" + e)) for e in ENGS}
        dsem = {e: [stack.enter_context(nc.semaphore("d_%s%d" % (e, i))) for i in range(NDS)]
                for e in ENGS if dcnt[e] > 0}

        def completion(i):
            op = ops[i]
            if op.dma:
                return dsem[op.eng][op.dj % NDS], 16 * (op.dj // NDS + 1)
            return sem[op.eng], op.ms

        per = {e: [i for i, op in enumerate(ops) if op.eng == e] for e in ENGS}
        final = {}
        for e in ENGS:
            if cnt[e] > 0:
                final[sem[e]] = cnt[e]
            for j in range(dcnt[e]):
                s = dsem[e][j % NDS]
                final[s] = max(final.get(s, 0), 16 * (j // NDS + 1))

        def run(e, eng):
            seen = {}
            for i in per[e]:
                op = ops[i]
                waits = {}
                for d in op.deps:
                    if e == 'pe' and ops[d].eng == 'pe':
                        continue
                    s, v = completion(d)
                    if waits.get(s, 0) < v:
                        waits[s] = v
                if op.dma and op.dj >= NDS:
                    s, v = completion(dlist[e][op.dj - NDS])
                    if waits.get(s, 0) < v:
                        waits[s] = v
                for s, v in waits.items():
                    if seen.get(s, 0) < v:
                        eng.wait_ge(s, v)
                        seen[s] = v
                ins = op.fn(eng)
                if op.dma:
                    ins.then_inc(dsem[e][op.dj % NDS], 16)
                elif op.sig:
                    ins.then_inc(sem[e], 1)
            if e == 'sp':
                for s, v in final.items():
                    if seen.get(s, 0) < v:
                        eng.wait_ge(s, v)

        with nc.Block() as block:
            @block.tensor
            def _(eng):
                run('pe', eng)

            @block.scalar
            def _(eng):
                run('act', eng)

            @block.vector
            def _(eng):
                run('dve', eng)

            @block.gpsimd
            def _(eng):
                run('pool', eng)

            @block.sync
            def _(eng):
                run('sp', eng)


class Arena:
    def __init__(self, t, n):
        self.t = t
        self.n = n
        self.off = 0

    def reset(self):
        self.off = 0

    def f32(self, *shape):
        n = int(np.prod(shape))
        a = self.t[:, self.off:self.off + n]
        self.off += n
        assert self.off <= self.n, ("arena overflow", self.off, self.n)
        if len(shape) == 2:
            a = a.rearrange("p (a b) -> p a b", b=shape[1])
        elif len(shape) == 3:
            a = a.rearrange("p (a b c) -> p a b c", b=shape[1], c=shape[2])
        return a

    def bf16(self, *shape):
        n = int(np.prod(shape))
        assert n % 2 == 0
        a = self.t[:, self.off:self.off + n // 2].bitcast(BF16)
        self.off += n // 2
        assert self.off <= self.n, ("arena overflow", self.off, self.n)
        if len(shape) == 2:
            a = a.rearrange("p (a b) -> p a b", b=shape[1])
        elif len(shape) == 3:
            a = a.rearrange("p (a b c) -> p a b c", b=shape[1], c=shape[2])
        elif len(shape) == 4:
            a = a.rearrange("p (a b c d) -> p a b c d", b=shape[1], c=shape[2], d=shape[3])
        return a


def build(cfg=None):
    cfg = cfg or {}
    dump = cfg.get('dump', ())
    stop_after = cfg.get('stop_after', 99)
    WINS = cfg.get('wins', (0, 1, 2))
    nc = bass.Bass("TRN2", target_bir_lowering=False)
    P = Prog()
    stack = ExitStack()

    def din(n, s, dt=F32):
        return nc.dram_tensor(n, list(s), dt, kind="ExternalInput").ap()

    def dscr(n, s, dt=BF16):
        kind = "ExternalOutput" if n in dump else "Internal"
        return nc.dram_tensor(n, list(s), dt, kind=kind).ap()

    xp = din("xp", [2, 2048, 2048])
    xsw = din("xsw", [3072, 2048])
    xsf = din("xsf", [8192, 2048])
    ccol_d = din("ccol", [128, 48])
    w_ada = din("w_ada", [2048, 12288])
    b_ada_col_d = din("b_ada_col", [128, 96])
    b_ada_row = din("b_ada_row", [1, 12288])
    gcols_d = din("gcols", [128, 48])
    grows = din("grows", [2, 2048])
    w_in = din("w_in", [2048, 4096])
    w_four = din("w_four", [4, 256, 256])
    w_out = din("w_out", [2048, 2048])
    w_mi = din("w_mi", [2048, 8192])
    w_mo = din("w_mo", [8192, 2048])
    tabc_d = din("tabc", [128, 1024], BF16)
    tabp = din("tabp", [4, 128, 2 * 16 * 512], BF16)
    tabs = din("tabs", [2, 8, 128, 2 * 8 * 512], BF16)
    dmask_d = din("dmask", [128, 256])
    edge_d = din("edge", [128, 8])
    identf_d = din("identf", [128, 128])
    identb_d = din("identb", [128, 128], BF16)

    yp = nc.dram_tensor("yp", [2, 2048, 2048], F32, kind="ExternalOutput").ap()
    ys = nc.dram_tensor("ys", [1024, 2048], F32, kind="ExternalOutput").ap()

    Wb_in_n = dscr("Wb_in", [2048, 4096])
    Wb_in = Wb_in_n.rearrange("(dc p) (pc n) -> pc p dc n", p=128, n=512)
    Wb_out = dscr("Wb_out", [4, 128, 16, 512])
    Wb_mi_n = dscr("Wb_mi", [2048, 8192])
    Wb_mi = Wb_mi_n.rearrange("(dc p) (fg n) -> fg p dc n", p=128, n=512)
    Wb_mo_n = dscr("Wb_mo", [8192, 2048])
    Wb_mo = Wb_mo_n.rearrange("(pcs f p) (nb n) -> nb pcs p f n", p=128, f=16, n=512)
    GP = dscr("GP", [3, 2, 2048], F32)
    SOWN = [2048, 2048, 1024]
    KLEN = [2048, 2048, 3072]
    QT = [dscr("QT%d" % w, [8, 128, SOWN[w]]) for w in range(3)]
    KT = [dscr("KT%d" % w, [8, 128, KLEN[w]]) for w in range(3)]
    VV = [dscr("VV%d" % w, [4096, 1024]) for w in range(3)]
    PQ = [dscr("PQ%d" % w, [[2048, 2048, 8192][w], 2048]) for w in range(3)]
    AT = [dscr("AT%d" % w, [1024, SOWN[w]]) for w in range(3)]
    FT = [dscr("FT%d" % w, [1024, SOWN[w]]) for w in range(3)]

    def sb(name, shape, dt):
        return stack.enter_context(nc.sbuf_tensor(name, list(shape), dt))

    identb = sb("identb_s", [128, 128], BF16)
    identf = sb("identf_s", [128, 128], F32)
    onesb = sb("onesb", [128, 128], BF16)
    dm = sb("dm", [128, 256], F32)
    edge = sb("edge_s", [128, 8], F32)
    cols_t = sb("cols", [128, 4 * 16 * 3], F32)
    cols = cols_t[:, :].rearrange("p (a b c) -> p a b c", a=4, b=16)
    gcols_t = sb("gcols_s", [128, 48], F32)
    gcols = gcols_t[:, :].rearrange("p (a b) -> p a b", b=16)
    AB_t = sb("AB", [128, 4 * 2 * 512], BF16)
    AB = AB_t[:, :].rearrange("p (g c n) -> p g c n", g=4, c=2)
    modcol_t = sb("modcol", [128, 288], F32)
    modcol = modcol_t[:, :].rearrange("p (a b c) -> p a b c", a=6, b=16)
    bcol_t = sb("bcol", [128, 96], F32)
    bcol = bcol_t[:, :].rearrange("p (a b) -> p a b", b=16)
    scT_t = sb("scT", [128, 48], BF16)
    scT = scT_t[:, :].rearrange("p (a b) -> p a b", b=3)
    ssqa = sb("ssqa", [128, 40], F32)
    epsc = sb("epsc", [128, 1], F32)
    ssqf = sb("ssqf", [128, 40], F32)
    rsa = sb("rsa", [128, 40], F32)
    rsf = sb("rsf", [128, 40], F32)
    NA = 48800
    arena_t = sb("arena", [128, NA], F32)
    ar = Arena(arena_t, NA)
    ps = [stack.enter_context(nc.psum_tensor("ps%d" % i, [128, 512], F32)) for i in range(8)]
    psb = [p[:, 0:256].bitcast(BF16) for p in ps]

    evac_ctr = [0]

    def evac_copy(out, in_, r, w):
        evac_ctr[0] += 1
        if evac_ctr[0] % 2 == 0:
            P.add('act', lambda e: e.activation(out=out, in_=in_, func=AF.Copy), r=r, w=w)
        else:
            P.add('dve', lambda e: e.tensor_copy(out=out, in_=in_), r=r, w=w)

    def mm_group(out, pairs, r, w):
        def fn(pe):
            n = len(pairs)
            ins = None
            for i, (l, rr) in enumerate(pairs):
                ins = pe.matmul(out, l, rr, start=(i == 0), stop=(i == n - 1))
            return ins
        P.add('pe', fn, r=r, w=w)

    def load(out, in_, w, r=(), eng='sp', nobar=False):
        P.add(eng, lambda e: e.dma_start(out=out, in_=in_), r=r, w=w, dma=True, nobar=nobar)

    def store(out, in_, r, w=(), eng='act'):
        P.add(eng, lambda e: e.dma_start(out=out, in_=in_), r=r, w=w, dma=True)

    load(identb[:, :], identb_d[:, :], w=['identb'])
    load(identf[:, :], identf_d[:, :], w=['identf'])
    load(dm[:, :], dmask_d[:, :], w=['dm'])
    load(edge[:, :], edge_d[:, :], w=['edge'])
    load(gcols_t[:, :], gcols_d[:, :], w=['gcols'])
    P.add('dve', lambda e: e.memset(onesb[:, :], 1.0), w=['onesb'])
    P.add('dve', lambda e: e.memset(epsc[:, :], EPS), w=['epsc'])
    P.add('dve', lambda e: e.memset(ssqa[:, :], 1.0), w=['ssqa'])
    P.add('dve', lambda e: e.memset(ssqf[:, :], 1.0), w=['ssqf'])

    for pc in range(8):
        load(Wb_in_n[pc * 256:(pc + 1) * 256, :], w_in[pc * 256:(pc + 1) * 256, :], w=[('Wb_in', pc)], eng='pool')

    ar.reset()
    zt = arena_t[:, NA - 1024:NA].bitcast(BF16).rearrange("p (a b) -> p a b", b=1024)
    P.add('dve', lambda e: e.memset(zt, 0.0), w=['zt'])

    cT = ar.f32(48)
    wa = [ar.bf16(16, 512) for _ in range(2)]
    wa32 = [ar.f32(16, 512) for _ in range(2)]
    load(cT, ccol_d[:, :], w=['cT'])
    P.add('dve', lambda e: e.memset(modcol, 0.0), w=['modcol'])
    load(bcol, b_ada_col_d[:, :].rearrange("p (a b) -> p a b", b=16), w=['bcol'])
    P.add('act', lambda e: e.activation(out=scT, in_=cT.rearrange("p (a b) -> p a b", b=3), func=AF.Silu),
          r=['cT'], w=['scT'])
    w_ada_v = w_ada.rearrange("(dc p) (j n) -> j p dc n", p=128, n=512)
    gstate = [0]

    def mod_block(j, wj, kw, pcol, pgate, gbufs):
        typ, jj = j // 4, j % 4
        if typ in (2, 5):
            gi = gstate[0]
            gstate[0] += 1
            gate = 0 if typ == 2 else 1
            pk = ('ps', pgate)
            pst = ps[pgate]
            mm_group(pst[0:3, :], [(scT[:, dc, :], wj[:, dc, :]) for dc in range(16)],
                     r=['scT', kw], w=[pk])
            br, gr, gt = gbufs[0][gi % 2], gbufs[1][gi % 2], gbufs[2][gi % 2]
            c0 = typ * 2048 + jj * 512
            load(br[0:3, :], b_ada_row[0:1, c0:c0 + 512].partition_broadcast(3)[:, 0, :], w=[('brow', gi % 2)])
            load(gr[0:3, :], grows[gate:gate + 1, jj * 512:(jj + 1) * 512].partition_broadcast(3)[:, 0, :],
                 w=[('grow', gi % 2)])
            P.add('dve', lambda e: e.tensor_tensor(out=gt[0:3, :], in0=pst[0:3, :], in1=br[0:3, :], op=ALU.add),
                  r=[pk, ('brow', gi % 2)], w=[('gtmp', gi % 2)])
            P.add('dve', lambda e: e.tensor_tensor(out=gt[0:3, :], in0=gt[0:3, :], in1=gr[0:3, :], op=ALU.mult),
                  r=[('grow', gi % 2)], w=[('gtmp', gi % 2), ('brow', gi % 2)])
            store(GP[:, gate, jj * 512:(jj + 1) * 512], gt[0:3, :], r=[('gtmp', gi % 2)], w=[('gtmp', gi % 2)])
        else:
            pk = ('ps', pcol)
            pst = ps[pcol]

            def fn(pe):
                ins = None
                for t in range(4):
                    for dc in range(16):
                        ins = pe.matmul(pst[:, t * 3:t * 3 + 3], wj[:, dc, t * 128:(t + 1) * 128], scT[:, dc, :],
                                        start=(dc == 0), stop=(dc == 15))
                return ins
            P.add('pe', fn, r=['scT', kw], w=[pk])
            P.add('dve', lambda e: e.tensor_copy(
                out=modcol[:, typ, jj * 4:(jj + 1) * 4, :], in_=pst[:, 0:12].rearrange("p (a b) -> p a b", b=3)),
                r=[pk], w=['modcol'])

    def mod_finish(t_sh, t_sc, k_g, k_sh, gsel, key):
        for typ in (t_sh, t_sc):
            P.add('dve', lambda e, typ=typ: e.tensor_tensor(out=modcol[:, typ], in0=modcol[:, typ],
                                                            in1=bcol[:, typ].unsqueeze(2).to_broadcast([128, 16, 3]), op=ALU.add),
                  r=['bcol'], w=['modcol'])
        P.add('dve', lambda e: e.scalar_tensor_tensor(
            out=cols[:, k_g], in0=modcol[:, t_sc], scalar=1.0,
            in1=gcols[:, gsel, :].unsqueeze(2).to_broadcast([128, 16, 3]), op0=ALU.add, op1=ALU.mult),
            r=['modcol', 'gcols'], w=[key])
        P.add('dve', lambda e: e.tensor_copy(out=cols[:, k_sh], in_=modcol[:, t_sh]), r=['modcol'], w=[key])

    for j in range(8):
        wj = wa[j % 2]
        kw = ('wa', j % 2)
        w32 = wa32[j % 2]
        k32 = ('wa32', j % 2)
        load(w32, w_ada_v[j], w=[k32])
        P.add('dve', lambda e, wj=wj, w32=w32: e.tensor_copy(out=wj[:, 0:8, :], in_=w32[:, 0:8, :]), r=[k32], w=[kw])
        P.add('act', lambda e, wj=wj, w32=w32: e.activation(out=wj[:, 8:16, :], in_=w32[:, 8:16, :], func=AF.Copy), r=[k32], w=[kw])
        mod_block(j, wj, kw, j % 2, 4, None)
    mod_finish(0, 1, 0, 1, 0, 'cols1')

    wf = ar.bf16(4, 2, 256)
    tcs = ar.bf16(2, 2, 256)
    load(wf, w_four.rearrange("g (cc p) e -> p g cc e", p=128), w=['wf'], eng='pool')
    load(tcs, tabc_d[:, :].rearrange("p (a b c) -> p a b c", a=2, b=2), w=['tcs'])
    for g in range(4):
        for cc in range(2):
            pk = ('ps', 6 + (g * 2 + cc) % 2)
            pst = ps[6 + (g * 2 + cc) % 2]

            def fn(pe, pst=pst, g=g, cc=cc):
                ins = None
                for s_ in range(2):
                    for c2 in range(2):
                        ins = pe.matmul(pst[:, s_ * 256:(s_ + 1) * 256], tcs[:, s_, c2, cc * 128:(cc + 1) * 128],
                                        wf[:, g, c2, :], start=(c2 == 0), stop=(c2 == 1))
                return ins
            P.add('pe', fn, r=['wf', 'tcs'], w=[pk])
            evac_copy(AB[:, g, cc, :], pst[:, :], r=[pk], w=['AB'])

    for fg in range(16):
        load(Wb_mi_n[fg * 128:(fg + 1) * 128, :], w_mi[fg * 128:(fg + 1) * 128, :], w=[('Wb_mi', fg)], eng='pool', nobar=True)
    for nt in range(16):
        load(Wb_mo_n[nt * 512:(nt + 1) * 512, :], w_mo[nt * 512:(nt + 1) * 512, :], w=[('Wb_mo', nt)], eng='pool', nobar=True)
    for w in WINS:
        regs = [(0, 1024), (3072, 4096)] if w < 2 else [(3072, 4096)]
        for (a, b) in regs:
            for r0 in range(a, b, 256):
                P.add('pool', lambda e, w=w, r0=r0: e.dma_start(out=VV[w][r0:r0 + 256, :].rearrange("(a p) n -> p a n", p=128), in_=zt),
                      r=['zt'], w=[('VVz', w, r0)], dma=True, nobar=True)
    P.barrier()

    if stop_after >= 1:
        ar.reset()
        xt = [ar.f32(2048) for _ in range(4)]
        xs = [ar.bf16(2048) for _ in range(4)]
        junk = ar.bf16(2048)
        hTs = [[ar.bf16(512) for _ in range(16)] for _ in range(2)]
        wb = [ar.bf16(16, 512) for _ in range(2)]
        wbu = [ar.bf16(16, 512) for _ in range(2)]
        wbu_loaded = [False]
        uT = [ar.bf16(512) for _ in range(8)]
        stq = [ar.bf16(512) for _ in range(4)]
        stv = ar.bf16(4, 1024)
        stpq = ar.bf16(4, 2048)
        st1 = ar.f32(8)
        blocks = []
        for w in WINS:
            if w < 2:
                for k in range(4):
                    blocks.append(dict(x=[xp[w, k * 512 + t * 128:k * 512 + (t + 1) * 128, :] for t in range(4)],
                                       b=w, w=w, q=k * 512, k=k * 512, v=1024 + k * 512, u=k * 512, pqw=w))
            else:
                for k in range(6):
                    blocks.append(dict(x=[xsw[k * 512 + t * 128:k * 512 + (t + 1) * 128, :] for t in range(4)],
                                       b=2, w=2, q=(k * 512 - 1024 if k in (2, 3) else None), k=k * 512, v=k * 512,
                                       u=None, pqw=2))
                for k in range(16):
                    blocks.append(dict(x=[xsf[k * 512 + t * 128:k * 512 + (t + 1) * 128, :] for t in range(4)],
                                       b=2, w=2, q=None, k=None, v=None, u=k * 512, pqw=2))
        wslot = [0]
        psr = [0]
        sq_i = [0]

        def nextps():
            i = 2 + psr[0] % 6
            psr[0] += 1
            return i

        def prep(bi_):
            blk = blocks[bi_]
            b = blk['b']
            hs = bi_ % 2
            hT = hTs[hs]
            for t in range(4):
                load(xt[t], blk['x'][t], w=[('xt', t)])
                P.add('act', lambda e, t=t: e.activation(out=junk, in_=xt[t], func=AF.Square, accum_out=st1[:, t:t + 1]),
                      r=[('xt', t)], w=['junk', ('st1', t)])
                P.add('act', lambda e, t=t: e.activation(out=st1[:, 4 + t:5 + t], in_=st1[:, t:t + 1], func=AF.Sqrt, scale=1.0 / 2048, bias=epsc[:, 0:1]),
                      r=[('st1', t)], w=[('st1b', t)])
                P.add('dve', lambda e, t=t: e.reciprocal(out=st1[:, 4 + t:5 + t], in_=st1[:, 4 + t:5 + t]),
                      r=[('st1b', t)], w=[('st1b', t)])
                P.add('dve', lambda e, t=t: e.tensor_scalar(out=xs[t], in0=xt[t], scalar1=st1[:, 4 + t:5 + t], scalar2=None,
                                                            op0=ALU.mult),
                      r=[('xt', t), ('st1b', t)], w=[('xs', t)])
            yield
            for dc in range(16):
                pi = dc % 2

                def fn(pe, dc=dc, pi=pi):
                    ins = None
                    for t in range(4):
                        ins = pe.transpose(psb[pi][:, t * 128:(t + 1) * 128], xs[t][:, dc * 128:(dc + 1) * 128], identb[:, :])
                    return ins
                P.add('pe', fn, r=[('xs', t) for t in range(4)] + ['identb'], w=[('ps', pi)])
                P.add('act', lambda e, dc=dc, pi=pi, b=b, hT=hT: e.activation(out=hT[dc], in_=psb[pi][:, 0:512], func=AF.Identity,
                                                                           bias=cols[:, 1, dc, b:b + 1], scale=cols[:, 0, dc, b:b + 1]),
                      r=[('ps', pi), 'cols1'], w=[('hT', hs, dc)])
                yield

        pgen = [None]

        def pstep():
            if pgen[0] is not None:
                try:
                    next(pgen[0])
                except StopIteration:
                    pgen[0] = None

        def pdrain():
            while pgen[0] is not None:
                pstep()

        pgen[0] = prep(0)
        pdrain()
        for bi_, blk in enumerate(blocks):
            b = blk['b']
            w = blk['w']
            hT = hTs[bi_ % 2]
            hkeys = [('hT', bi_ % 2, dc) for dc in range(16)]
            need = []
            if blk['q'] is not None:
                need += [0, 1]
            if blk['k'] is not None:
                need += [2, 3]
            if blk['v'] is not None:
                need += [4, 5]
            if blk['u'] is not None:
                need += [6, 7]
            pdrain()
            if bi_ + 1 < len(blocks):
                pgen[0] = prep(bi_ + 1)
            for pci, pc in enumerate(need):
                if need == [6, 7]:
                    wcur = wbu[pc - 6]
                    wk = ('wbu', pc - 6)
                    if not wbu_loaded[0]:
                        load(wcur, Wb_in[pc], w=[wk])
                        if pc == 7:
                            wbu_loaded[0] = True
                else:
                    sl = wslot[0] % 2
                    wslot[0] += 1
                    wcur = wb[sl]
                    wk = ('wb', sl)
                    load(wcur, Wb_in[pc], w=[wk])
                if pc < 4 or pc >= 6:
                    for t in range(4):
                        pi = nextps()
                        mm_group(ps[pi][:, :], [(wcur[:, dc, t * 128:(t + 1) * 128], hT[dc]) for dc in range(16)],
                                 r=hkeys + [wk], w=[('ps', pi)])
                        pstep()
                        if pc < 4:
                            h = (pc % 2) * 4 + t
                            si = sq_i[0] % 4
                            sq_i[0] += 1
                            evac_copy(stq[si], ps[pi][:, :], r=[('ps', pi)], w=[('stq', si)])
                            if pc < 2:
                                dst = QT[w][h, :, blk['q']:blk['q'] + 512]
                            else:
                                dst = KT[w][h, :, blk['k']:blk['k'] + 512]
                            store(dst, stq[si], r=[('stq', si)], w=[('stq', si)])
                        else:
                            ct = (pc - 6) * 4 + t
                            evac_copy(uT[ct], ps[pi][:, :], r=[('ps', pi)], w=[('uT', ct)])
                            if ct % 2 == 1:
                                g = ct // 2
                                for tt in range(4):
                                    pj = nextps()
                                    mm_group(ps[pj][:, :], [(uT[2 * g + cc][:, tt * 128:(tt + 1) * 128], AB[:, g, cc, :])
                                                            for cc in range(2)],
                                             r=[('uT', 2 * g), ('uT', 2 * g + 1), 'AB'], w=[('ps', pj)])
                                    pstep()
                                    o = stpq[:, tt, g * 512:(g + 1) * 512].rearrange("p (j q n) -> p q j n", j=2, q=2)
                                    i_ = ps[pj][:, :].rearrange("p (q j n) -> p q j n", q=2, j=2)
                                    evac_copy(o, i_, r=[('ps', pj)], w=[('stpq', tt)])
                    if pc == 7:
                        u0 = blk['u']
                        store(PQ[blk['pqw']][u0:u0 + 512, :].rearrange("(t p) n -> p t n", p=128), stpq,
                              r=[('stpq', tt) for tt in range(4)], w=[('stpq', tt) for tt in range(4)])
                else:
                    half = pc - 4
                    for tt in range(4):
                        pi = nextps()
                        mm_group(ps[pi][:, :], [(hT[dc][:, tt * 128:(tt + 1) * 128], wcur[:, dc, :]) for dc in range(16)],
                                 r=hkeys + [wk], w=[('ps', pi)])
                        pstep()
                        evac_copy(stv[:, tt, half * 512:(half + 1) * 512], ps[pi][:, :], r=[('ps', pi)], w=[('stv', tt)])
                    if pc == 5:
                        v0 = blk['v']
                        store(VV[w][v0:v0 + 512, :].rearrange("(t p) n -> p t n", p=128), stv,
                              r=[('stv', tt) for tt in range(4)], w=[('stv', tt) for tt in range(4)])
        P.barrier()

    if stop_after >= 2:
        ar.reset()
        kTb = [ar.bf16(4096) for _ in range(2)]
        qTb = [ar.bf16(2048) for _ in range(2)]
        NCH = {1: 17, 4: 20, 16: 32}
        Vh = [{d: ar.bf16(NCH[d], 128) for d in (1, 4, 16)} for _ in range(2)]
        accden = [ar.f32(2, 2048) for _ in range(2)]
        sbt = [ar.f32(2, 128) for _ in range(4)]
        pT = [ar.bf16(2, 128) for _ in range(4)]
        ast = [ar.bf16(2048) for _ in range(2)]
        sqb = [ar.bf16(2048) for _ in range(2)]
        for i in range(2):
            P.add('dve', lambda e, i=i: e.memset(kTb[i], 0.0), w=[('kT', i)])
        wa_bg = [ar.bf16(16, 512) for _ in range(2)]
        gb = ([ar.f32(512) for _ in range(2)], [ar.f32(512) for _ in range(2)], [ar.f32(512) for _ in range(2)])
        wo32 = [ar.f32(2048) for _ in range(2)]
        wo16 = [ar.bf16(2048) for _ in range(2)]
        bg_tasks = []

        def t_mod(j):
            wj = wa_bg[j % 2]
            kw = ('wabg', j % 2)
            load(wj, w_ada_v[j], w=[kw], eng='pool')
            mod_block(j, wj, kw, j % 2, 2 + j % 2, gb)

        def t_wo(kc):
            i = kc % 2
            load(wo32[i], w_out[kc * 128:(kc + 1) * 128, :], w=[('wo32', i)])
            P.add('dve', lambda e: e.tensor_scalar(out=wo16[i], in0=wo32[i], scalar1=gcols[:, 2, kc:kc + 1],
                                                   scalar2=None, op0=ALU.mult),
                  r=[('wo32', i), 'gcols'], w=[('wo16', i)])
            store(Wb_out[:, :, kc, :].rearrange("nb p n -> p nb n"), wo16[i].rearrange("p (a b) -> p a b", b=512),
                  r=[('wo16', i)], w=[('wo16', i)])
        for i in range(16):
            bg_tasks.append((t_mod, 8 + i))
            bg_tasks.append((t_wo, i))
        hb = 0
        tile_ctr = 0
        tbase = 0
        for w in WINS:
            S = SOWN[w]
            sample = (w == 2)
            ntt = S // 128
            tbase = 16 * w
            for h in range(8):
                bi = hb % 2
                hb += 1
                kT, qT, V_, AD = kTb[bi], qTb[bi], Vh[bi], accden[bi]
                vz_keys = [('VVz', w, r0) for r0 in ((list(range(0, 1024, 256)) + list(range(3072, 4096, 256))) if w < 2 else list(range(3072, 4096, 256)))]
                if sample:
                    load(kT[:, 0:3072], KT[w][h], w=[('kT', bi)])
                else:
                    load(kT[:, 1024:3072], KT[w][h], w=[('kT', bi)])
                load(qT[:, 0:S], QT[w][h], w=[('qT', bi)])
                for d in (1, 4, 16):
                    L = S // d
                    Lh = 1024 // d
                    nch = (L + 128 + 127) // 128
                    for r_ in range(d):
                        t0 = (Lh - 64) * d + r_
                        src = VV[w][t0:t0 + (nch * 128 - 1) * d + 1:d, h * 128:(h + 1) * 128].rearrange("(m k) n -> k m n", k=128)
                        load(V_[d][:, r_ * nch:(r_ + 1) * nch, :], src, w=[('V', bi, d, r_)], r=vz_keys)
                tiles = []
                for bidx, d in enumerate((1, 4, 16)):
                    L = S // d
                    nq = min(128, L)
                    for r_ in range(d):
                        for n in range(L // nq):
                            tiles.append((bidx, d, r_, n))
                LA = 3

                def stage1(tl, ti):
                    bidx, d, r_, n = tl
                    L = S // d
                    Lh = 1024 // d
                    nq = min(128, L)
                    ntile = L // nq
                    coef = -(2.0 ** -(h + 1)) * d / SC
                    pS = ps[ti]
                    kS = ('ps', ti)
                    qcols = slice(r_ + 128 * n * d, r_ + 128 * n * d + (nq - 1) * d + 1, d)
                    qap = qT[:, qcols]
                    kT_ = kT

                    def fn(pe, kT=kT_):
                        ins = None
                        for c in range(2):
                            k0 = (Lh - 64 + 128 * (n + c)) * d + r_
                            ins = pe.matmul(pS[:, c * 128:c * 128 + nq], kT[:, k0:k0 + 127 * d + 1:d], qap,
                                            start=True, stop=True)
                        return ins
                    P.add('pe', fn, r=[('kT', bi), ('qT', bi)], w=[kS])
                    sbv = sbt[ti][:, :, 0:nq]
                    P.add('dve', lambda e: e.scalar_tensor_tensor(
                        out=sbv, in0=dm[:, :].rearrange("p (c n) -> p c n", c=2)[:, :, 0:nq], scalar=coef,
                        in1=pS[:, 0:256].rearrange("p (c n) -> p c n", c=2)[:, :, 0:nq], op0=ALU.mult, op1=ALU.add),
                        r=[kS, 'dm'], w=[('sb', ti)])
                    ecol = [0, 0]
                    if n == 0:
                        ecol[0] = 3 if sample else 1
                    if n == ntile - 1:
                        if sample:
                            ecol[1] = 5 if d == 16 else 4
                        else:
                            ecol[1] = 2
                    pTv = pT[ti][:, :, 0:nq]
                    if ecol == [0, 0]:
                        P.add('act', lambda e: e.activation(out=pTv, in_=sbv, func=AF.Exp, scale=SC),
                              r=[('sb', ti)], w=[('pT', ti)])
                    else:
                        for c in range(2):
                            P.add('act', lambda e, c=c, ec=ecol[c]: e.activation(
                                out=pTv[:, c, :], in_=sbv[:, c, :], func=AF.Exp, scale=SC, bias=edge[:, ec:ec + 1]),
                                r=[('sb', ti), 'edge'], w=[('pT', ti)])

                def stage2(tl, ti):
                    bidx, d, r_, n = tl
                    L = S // d
                    nch = (L + 128 + 127) // 128
                    nq = min(128, L)
                    pO = ps[4 + ti]
                    kO = ('ps', 4 + ti)
                    pTv = pT[ti][:, :, 0:nq]
                    qcols = slice(r_ + 128 * n * d, r_ + 128 * n * d + (nq - 1) * d + 1, d)

                    Vd = V_[d]

                    def fn2(pe):
                        ins = None
                        for c in range(2):
                            ins = pe.matmul(pO[:, 0:nq], Vd[:, r_ * nch + n + c, :], pTv[:, c, :],
                                            start=(c == 0), stop=(c == 1))
                        for c in range(2):
                            ins = pe.matmul(pO[:, 128:128 + nq], onesb[:, :], pTv[:, c, :],
                                            start=(c == 0), stop=(c == 1))
                        return ins
                    P.add('pe', fn2, r=[('V', bi, d, r_), ('pT', ti), 'onesb'], w=[kO])
                    oap = AD[:, :, qcols]
                    iap = pO[:, 0:256].rearrange("p (c n) -> p c n", c=2)[:, :, 0:nq]
                    if bidx == 0:
                        P.add('act', lambda e: e.activation(out=oap, in_=iap, func=AF.Copy),
                              r=[kO], w=[('AD', bi)])
                    else:
                        P.add('dve', lambda e: e.tensor_tensor(out=oap, in0=iap, in1=oap, op=ALU.add),
                              r=[kO], w=[('AD', bi)])

                slots = []
                for s_ in range(len(tiles) + LA):
                    if s_ < len(tiles):
                        ti = tile_ctr % 4
                        tile_ctr += 1
                        slots.append(ti)
                        stage1(tiles[s_], ti)
                    if s_ - LA >= 0:
                        stage2(tiles[s_ - LA], slots[s_ - LA])
                P.add('act', lambda e, AD=AD, S=S: e.activation(out=AD[:, 1, 0:S], in_=AD[:, 1, 0:S], func=AF.Ln), r=[], w=[('AD', bi)])
                P.add('act', lambda e, AD=AD, S=S: e.activation(out=AD[:, 1, 0:S], in_=AD[:, 1, 0:S], func=AF.Exp, scale=-1.0), r=[], w=[('AD', bi)])
                P.add('dve', lambda e, AD=AD, S=S: e.tensor_tensor(out=AD[:, 0, 0:S], in0=AD[:, 0, 0:S], in1=AD[:, 1, 0:S], op=ALU.mult),
                      r=[], w=[('AD', bi)])
                P.add('act', lambda e, AD=AD, S=S, bi=bi: e.activation(out=ast[bi][:, 0:S], in_=AD[:, 0, 0:S], func=AF.Copy),
                      r=[('AD', bi)], w=[('ast', bi)])
                P.add('dve', lambda e, AD=AD, S=S, bi=bi: e.tensor_tensor(out=sqb[bi][:, 0:S], in0=AD[:, 0, 0:S], in1=AD[:, 0, 0:S], op=ALU.mult),
                      r=[('AD', bi)], w=[('sqb', bi)])
                store(AT[w][h * 128:(h + 1) * 128, :], ast[bi][:, 0:S], r=[('ast', bi)], w=[('ast', bi)])
                for _ in range(2):
                    if bg_tasks:
                        f_, a_ = bg_tasks.pop(0)
                        f_(a_)

                def fn3(pe, bi=bi, ntt=ntt):
                    ins = None
                    for tt in range(ntt):
                        ins = pe.matmul(ps[4][:, tt:tt + 1], sqb[bi][:, tt * 128:(tt + 1) * 128], onesb[:, 0:1],
                                        start=True, stop=True)
                    return ins
                P.add('pe', fn3, r=[('sqb', bi), 'onesb'], w=[('ps', 4)])
                if h == 0:
                    P.add('dve', lambda e, ntt=ntt, tbase=tbase: e.tensor_copy(out=ssqa[:, tbase:tbase + ntt], in_=ps[4][:, 0:ntt]),
                          r=[('ps', 4)], w=['ssqa'])
                else:
                    P.add('dve', lambda e, ntt=ntt, tbase=tbase: e.tensor_tensor(out=ssqa[:, tbase:tbase + ntt], in0=ps[4][:, 0:ntt],
                                                                                   in1=ssqa[:, tbase:tbase + ntt], op=ALU.add),
                          r=[('ps', 4)], w=['ssqa'])
            tbase += ntt
        while bg_tasks:
            f_, a_ = bg_tasks.pop(0)
            f_(a_)
        mod_finish(3, 4, 2, 3, 1, 'cols2')
        P.barrier()

    if stop_after >= 3:
        ar.reset()
        fst = [ar.bf16(512) for _ in range(8)]
        fsq = [ar.bf16(512) for _ in range(8)]
        base = ar.off
        tbase = 0
        pass_ctr = 0
        slc = 0
        for w in WINS:
            S = SOWN[w]
            ntt = S // 128
            nj = S // 512
            tbase = 16 * w
            ar.off = base
            if w < 2:
                pqr = ar.bf16(16, 2048)
                tb = [ar.bf16(2, 16, 512) for _ in range(2)]
                for ch in range(16):
                    load(pqr[:, ch, :], PQ[w][ch * 128:(ch + 1) * 128, :], w=[('pqr', ch)])
                pq5 = pqr.rearrange("p c (e q n) -> p c e q n", e=8, q=2)
            else:
                P.barrier()
                pqs = [ar.bf16(8, 1024) for _ in range(2)]
                tbs = [ar.bf16(2, 8, 512) for _ in range(2)]
            for j in range(nj):
                if w < 2:
                    tj = tb[j % 2]
                    tk = ('tb', j % 2)
                    tpv = tabp[j].rearrange("p (a b c) -> p a b c", a=2, b=16)
                    for cs_ in range(2):
                        for hh in range(2):
                            load(tj[:, cs_, hh * 8:(hh + 1) * 8, :], tpv[:, cs_, hh * 8:(hh + 1) * 8, :], w=[tk])
                for half in range(2):
                    bset = (pass_ctr % 2) * 4
                    pass_ctr += 1
                    if w < 2:
                        for e4 in range(4):
                            et = half * 4 + e4
                            pi = bset + e4
                            pairs = []
                            for ch in range(16):
                                pairs.append((pq5[:, ch, et, 0, :], tj[:, 0, ch, :]))
                                pairs.append((pq5[:, ch, et, 1, :], tj[:, 1, ch, :]))
                            mm_group(ps[pi][:, :], pairs, r=[('pqr', ch) for ch in range(16)] + [tk], w=[('ps', pi)])
                    else:
                        for grp in range(8):
                            i = slc % 2
                            slc += 1
                            src_ = PQ[2][grp * 1024:(grp + 1) * 1024, half * 1024:(half + 1) * 1024].rearrange("(c p) n -> p c n", p=128)
                            load(pqs[i], src_, w=[('pqs', i)])
                            load(tbs[i], tabs[j, grp].rearrange("p (a b c) -> p a b c", a=2, b=8), w=[('tbs', i)])
                            pv = pqs[i].rearrange("p c (e q n) -> p c e q n", e=4, q=2)
                            for e4 in range(4):
                                pi = bset + e4

                                def fn(pe, pv=pv, tt_=tbs[i], e4=e4, pi=pi, grp=grp):
                                    ins = None
                                    for ch in range(8):
                                        for q_ in range(2):
                                            ins = pe.matmul(ps[pi][:, :], pv[:, ch, e4, q_, :], tt_[:, q_, ch, :],
                                                            start=(grp == 0 and ch == 0 and q_ == 0),
                                                            stop=(grp == 7 and ch == 7 and q_ == 1))
                                    return ins
                                P.add('pe', fn, r=[('pqs', i), ('tbs', i)], w=[('ps', pi)])
                    for e4 in range(4):
                        et = half * 4 + e4
                        pi = bset + e4
                        P.add('dve', lambda e, et=et, pi=pi: e.tensor_copy(out=fst[et], in_=ps[pi][:, :]),
                              r=[('ps', pi)], w=[('fst', et)])
                        P.add('act', lambda e, et=et, pi=pi: e.activation(out=fsq[et], in_=fst[et], func=AF.Square),
                              r=[('fst', et)], w=[('fsq', et)])
                        store(FT[w][et * 128:(et + 1) * 128, j * 512:(j + 1) * 512], fst[et], r=[('fst', et)], w=[('fst', et)])
                pq_ = bset

                def fnq(pe, pq_=pq_):
                    ins = None
                    for tt in range(4):
                        for et in range(8):
                            ins = pe.matmul(ps[pq_][:, tt:tt + 1], fsq[et][:, tt * 128:(tt + 1) * 128], onesb[:, 0:1],
                                            start=(et == 0), stop=(et == 7))
                    return ins
                P.add('pe', fnq, r=[('fsq', et) for et in range(8)] + ['onesb'], w=[('ps', pq_)])
                c0 = tbase + j * 4
                P.add('dve', lambda e, pq_=pq_, c0=c0: e.tensor_copy(out=ssqf[:, c0:c0 + 4], in_=ps[pq_][:, 0:4]),
                      r=[('ps', pq_)], w=['ssqf'])
            tbase += ntt
        P.barrier()

    if stop_after >= 4:
        ar.reset()
        NT = 40
        for (src_, dst_, nm) in ((ssqa, rsa, 'rsa'), (ssqf, rsf, 'rsf')):
            P.add('act', lambda e, src_=src_, dst_=dst_: e.activation(out=dst_[:, 0:NT], in_=src_[:, 0:NT], func=AF.Sqrt, scale=1.0 / 1024, bias=epsc[:, 0:1]), w=[nm])
            P.add('dve', lambda e, dst_=dst_: e.reciprocal(out=dst_[:, 0:NT], in_=dst_[:, 0:NT]), r=[nm], w=[nm])
        xt = [ar.f32(2048) for _ in range(4)]
        r1o = ar.off
        aT = ar.bf16(8, 512)
        fT = ar.bf16(8, 512)
        gp1 = ar.f32(2048)
        y1 = [ar.f32(2048) for _ in range(4)]
        tmpA = [ar.f32(512) for _ in range(2)]
        ar.off = r1o
        hid = [ar.bf16(512) for _ in range(64)]

        def R1(a, n):
            return [('R1', i) for i in range(a, a + n)]
        k_aT, k_fT, k_gp1 = R1(0, 8), R1(8, 8), R1(16, 8)
        k_y1 = [R1(24 + 8 * t, 8) for t in range(4)]
        k_tmpA = [R1(56, 2), R1(58, 2)]
        y2 = [ar.f32(2048) for _ in range(4)]
        xs2 = [y2[t][:, 0:1024].bitcast(BF16) for t in range(4)]
        WS = [ar.bf16(8192) for _ in range(2)]
        h2o = ar.off
        h2T = [ar.bf16(512) for _ in range(16)]
        ar.off = h2o
        y2T = [ar.f32(512) for _ in range(4)]
        ar.off = h2o + 4096
        gp2 = ar.f32(2048)
        rtmp = [ar.f32(512) for _ in range(2)]
        st3 = ar.f32(16)
        wsc = [0]
        psr3 = [0]
        rl = [0]
        blocks3 = []
        tb_ = {0: 0, 1: 16, 2: 32}
        for w in WINS:
            for k in range(SOWN[w] // 512):
                blocks3.append((w, k * 512))

        def ws_load(src_ap, shape3, extra_r=()):
            i = wsc[0] % 2
            wsc[0] += 1
            v = WS[i].rearrange("p (a b) -> p a b", b=shape3[1])
            load(v, src_ap, w=[('WS', i)], r=list(extra_r))
            return v, ('WS', i)

        for (w, tok0) in blocks3:
            b = w
            tile0 = tb_[w] + tok0 // 128
            load(aT, AT[w][:, tok0:tok0 + 512].rearrange("(kc p) n -> p kc n", p=128), w=k_aT)
            load(fT, FT[w][:, tok0:tok0 + 512].rearrange("(kc p) n -> p kc n", p=128), w=k_fT)
            pre_w = ws_load(Wb_out[0], (16, 512))
            load(gp1, GP[b, 0:1, :].partition_broadcast(128)[:, 0, :], w=k_gp1)
            for t in range(4):
                if w < 2:
                    xsrc = xp[w, tok0 + t * 128:tok0 + (t + 1) * 128, :]
                else:
                    xsrc = xsw[1024 + tok0 + t * 128:1024 + tok0 + (t + 1) * 128, :]
                load(xt[t], xsrc, w=[('xt', t)])
            load(gp2, GP[b, 1:2, :].partition_broadcast(128)[:, 0, :], w=['gp2'])
            for nb in range(4):
                wv, wk = pre_w if nb == 0 else ws_load(Wb_out[nb], (16, 512))
                for tt in range(4):
                    pa = psr3[0] % 8
                    pb = (psr3[0] + 1) % 8
                    psr3[0] += 2
                    mm_group(ps[pa][:, :], [(aT[:, kc, tt * 128:(tt + 1) * 128], wv[:, kc, :]) for kc in range(8)],
                             r=k_aT + [wk], w=[('ps', pa)])
                    mm_group(ps[pb][:, :], [(fT[:, kc, tt * 128:(tt + 1) * 128], wv[:, 8 + kc, :]) for kc in range(8)],
                             r=k_fT + [wk], w=[('ps', pb)])
                    ti = (nb * 4 + tt) % 2
                    tl = tile0 + tt
                    P.add('act', lambda e, ti=ti, pa=pa, tl=tl: e.activation(out=tmpA[ti], in_=ps[pa][:, :], func=AF.Copy,
                                                                            scale=rsa[:, tl:tl + 1]),
                          r=[('ps', pa), 'rsa'], w=k_tmpA[ti])
                    P.add('dve', lambda e, ti=ti, pb=pb, tl=tl, tt=tt, nb=nb: e.scalar_tensor_tensor(
                        out=y1[tt][:, nb * 512:(nb + 1) * 512], in0=ps[pb][:, :], scalar=rsf[:, tl:tl + 1], in1=tmpA[ti],
                        op0=ALU.mult, op1=ALU.add),
                        r=[('ps', pb), 'rsf'] + k_tmpA[ti], w=k_y1[tt])
            for t in range(4):
                P.add('act', lambda e, t=t: e.activation(out=y2[t], in_=y1[t], func=AF.Square, accum_out=st3[:, t:t + 1]),
                      r=k_y1[t], w=[('y2', t), ('st3', t)])
                P.add('act', lambda e, t=t: e.activation(out=st3[:, t:t + 1], in_=st3[:, t:t + 1], func=AF.Sqrt, scale=1.0 / 2048, bias=epsc[:, 0:1]), r=[], w=[('st3', t)])
                P.add('dve', lambda e, t=t: e.reciprocal(out=st3[:, t:t + 1], in_=st3[:, t:t + 1]), r=[], w=[('st3', t)])
                P.add('dve', lambda e, t=t: e.scalar_tensor_tensor(out=y1[t], in0=y1[t], scalar=st3[:, t:t + 1], in1=gp1,
                                                                   op0=ALU.mult, op1=ALU.mult),
                      r=k_gp1 + [('st3', t)], w=k_y1[t])
                P.add('pool' if t % 2 == 0 else 'dve', lambda e, t=t: e.tensor_tensor(out=xt[t], in0=y1[t], in1=xt[t], op=ALU.add),
                      r=k_y1[t], w=[('xt', t)])
            for t in range(4):
                P.add('act', lambda e, t=t: e.activation(out=y2[t], in_=xt[t], func=AF.Square, accum_out=st3[:, 4 + t:5 + t]),
                      r=[('xt', t)], w=[('y2', t), ('st3b', t)])
                P.add('act', lambda e, t=t: e.activation(out=st3[:, 4 + t:5 + t], in_=st3[:, 4 + t:5 + t], func=AF.Sqrt, scale=1.0 / 2048, bias=epsc[:, 0:1]), r=[], w=[('st3b', t)])
                P.add('dve', lambda e, t=t: e.reciprocal(out=st3[:, 4 + t:5 + t], in_=st3[:, 4 + t:5 + t]), r=[], w=[('st3b', t)])
                P.add('dve', lambda e, t=t: e.tensor_scalar(out=xs2[t], in0=xt[t], scalar1=st3[:, 4 + t:5 + t], scalar2=None,
                                                            op0=ALU.mult),
                      r=[('xt', t), ('st3b', t)], w=[('y2', t)])
            for dc in range(16):
                pi = dc % 4

                def fn(pe, dc=dc, pi=pi):
                    ins = None
                    for t in range(4):
                        ins = pe.transpose(psb[pi][:, t * 128:(t + 1) * 128], xs2[t][:, dc * 128:(dc + 1) * 128], identb[:, :])
                    return ins
                P.add('pe', fn, r=[('y2', t) for t in range(4)] + ['identb'], w=[('ps', pi)])
                P.add('act', lambda e, dc=dc, pi=pi, b=b: e.activation(out=h2T[dc], in_=psb[pi][:, 0:512], func=AF.Identity,
                                                                     bias=cols[:, 3, dc, b:b + 1], scale=cols[:, 2, dc, b:b + 1]),
                      r=[('ps', pi), 'cols2'], w=[('h2T', dc)])
            h2keys = [('h2T', dc) for dc in range(16)]
            for fg in range(16):
                wv, wk = ws_load(Wb_mi[fg], (16, 512), extra_r=[('Wb_mi', q_) for q_ in range(16)])
                for t in range(4):
                    ft = fg * 4 + t
                    pi = 4 + psr3[0] % 4
                    psr3[0] += 1
                    mm_group(ps[pi][:, :], [(wv[:, dc, t * 128:(t + 1) * 128], h2T[dc]) for dc in range(16)],
                             r=h2keys + [wk], w=[('ps', pi)])
                    ri = rl[0] % 2
                    rl[0] += 1
                    P.add('act', lambda e, ri=ri, pi=pi: e.activation(out=rtmp[ri], in_=ps[pi][:, :], func=AF.Relu),
                          r=[('ps', pi)], w=[('rtmp', ri)])
                    if ft % 2 == 0:
                        P.add('dve', lambda e, ri=ri, ft=ft: e.tensor_tensor(out=hid[ft], in0=rtmp[ri], in1=rtmp[ri], op=ALU.mult),
                              r=[('rtmp', ri)], w=R1(ft, 1))
                    else:
                        P.add('pool', lambda e, ri=ri, ft=ft: e.tensor_tensor(out=hid[ft], in0=rtmp[ri], in1=rtmp[ri], op=ALU.mult),
                              r=[('rtmp', ri)], w=R1(ft, 1))
            hkeys = R1(0, 64)
            mo_keys = [('Wb_mo', q_) for q_ in range(16)]
            for nb in range(4):
                bb = 4 * (nb % 2)
                for pcs in range(4):
                    wv, wk = ws_load(Wb_mo[nb, pcs], (16, 512), extra_r=mo_keys)
                    for tt in range(4):
                        def fnm(pe, wv=wv, tt=tt, pcs=pcs, bb=bb):
                            ins = None
                            for f16 in range(16):
                                ins = pe.matmul(ps[bb + tt][:, :], hid[pcs * 16 + f16][:, tt * 128:(tt + 1) * 128], wv[:, f16, :],
                                                start=(pcs == 0 and f16 == 0), stop=(pcs == 3 and f16 == 15))
                            return ins
                        P.add('pe', fnm, r=R1(pcs * 16, 16) + [wk], w=[('ps', bb + tt)])
                for tt in range(4):
                    evac_copy(y2[tt][:, nb * 512:(nb + 1) * 512], ps[bb + tt][:, :], r=[('ps', bb + tt)], w=[('y2', tt)])
            for t in range(4):
                P.add('act', lambda e, t=t: e.activation(out=y1[t], in_=y2[t], func=AF.Square, accum_out=st3[:, 8 + t:9 + t]),
                      r=[('y2', t)], w=k_y1[t] + [('st3c', t)])
                P.add('act', lambda e, t=t: e.activation(out=st3[:, 8 + t:9 + t], in_=st3[:, 8 + t:9 + t], func=AF.Sqrt, scale=1.0 / 2048, bias=epsc[:, 0:1]), r=[], w=[('st3c', t)])
                P.add('dve', lambda e, t=t: e.reciprocal(out=st3[:, 8 + t:9 + t], in_=st3[:, 8 + t:9 + t]), r=[], w=[('st3c', t)])
                P.add('dve', lambda e, t=t: e.scalar_tensor_tensor(out=y2[t], in0=y2[t], scalar=st3[:, 8 + t:9 + t], in1=gp2,
                                                                   op0=ALU.mult, op1=ALU.mult),
                      r=['gp2', ('st3c', t)], w=[('y2', t)])
                P.add('pool' if t % 2 == 0 else 'dve', lambda e, t=t: e.tensor_tensor(out=y2[t], in0=y2[t], in1=xt[t], op=ALU.add),
                      r=[('xt', t)], w=[('y2', t)])
                if w < 2:
                    dst = yp[w, tok0 + t * 128:tok0 + (t + 1) * 128, :]
                else:
                    dst = ys[tok0 + t * 128:tok0 + (t + 1) * 128, :]
                store(dst, y2[t], r=[('y2', t)], w=[('y2', t)], eng=('pool' if t % 2 == 0 else 'act'))
    return nc, P, stack


_CONST = {}


def _bf(a):
    return np.ascontiguousarray(a.astype(np.float32)).astype(ml_dtypes.bfloat16)


def _consts():
    if _CONST:
        return _CONST
    C = _CONST
    C['identf'] = np.eye(128, dtype=np.float32)
    C['identb'] = _bf(np.eye(128))
    kk = np.arange(128)[:, None]
    ii = np.arange(128)[None, :]
    d0 = np.where(kk >= ii, np.abs(ii + 64 - kk), DBIG).astype(np.float32)
    d1 = np.where(kk <= ii, np.abs(ii - 64 - kk), DBIG).astype(np.float32)
    C['dmask'] = np.ascontiguousarray(np.concatenate([d0, d1], axis=1))
    cp = (np.arange(2)[None, :, None] * 128 + np.arange(128)[:, None, None])
    c = np.arange(256)[None, None, :]
    ang = 2 * np.pi * ((cp * c) % 256) / 256.0
    tc = np.stack([np.cos(ang) / 16.0, -np.sin(ang) / 16.0], axis=1)
    C['tabc'] = _bf(tc.reshape(128, 1024))
    S = 2048
    lut_c = np.cos(2 * np.pi * np.arange(S) / S) / np.sqrt(S)
    lut_s = np.sin(2 * np.pi * np.arange(S) / S) / np.sqrt(S)
    s = (np.arange(16)[None, :, None] * 128 + np.arange(128)[:, None, None])
    tabp = np.empty((4, 128, 2, 16, 512), dtype=np.float32)
    for j in range(4):
        sp_ = j * 512 + np.arange(512)[None, None, :]
        k = (s * sp_) % S
        tabp[j, :, 0] = lut_c[k]
        tabp[j, :, 1] = lut_s[k]
    C['tabp'] = _bf(tabp.reshape(4, 128, 2 * 16 * 512))
    S = 8192
    lut_c = (np.cos(2 * np.pi * np.arange(S) / S) / np.sqrt(S)).astype(np.float32)
    lut_s = (np.sin(2 * np.pi * np.arange(S) / S) / np.sqrt(S)).astype(np.float32)
    tabs_all = []
    for core in range(8):
        t = np.empty((2, 8, 128, 2, 8, 512), dtype=np.float32)
        for j in range(2):
            sp_ = 1024 * core + j * 512 + np.arange(512, dtype=np.int64)[None, None, :]
            for grp in range(8):
                s = ((grp * 8 + np.arange(8, dtype=np.int64))[None, :, None] * 128 + np.arange(128, dtype=np.int64)[:, None, None])
                k = (s * sp_) % S
                t[j, grp, :, 0] = lut_c[k]
                t[j, grp, :, 1] = lut_s[k]
        tabs_all.append(_bf(t.reshape(2, 8, 128, 2 * 8 * 512)))
    C['tabs'] = tabs_all
    edges = []
    for core in range(8):
        vl = NEG if core == 0 else 0.0
        vr = NEG if core == 7 else 0.0
        e = np.zeros((128, 8), dtype=np.float32)
        e[:64, 1] = NEG
        e[64:, 2] = NEG
        e[:64, 3] = vl
        e[64:, 4] = vr
        e[:64, 5] = vr
        e[64:, 5] = NEG
        edges.append(e)
    C['edge'] = edges
    return C


def make_in_maps(inputs, n_cores=8):
    C = _consts()
    f32 = lambda a: np.ascontiguousarray(np.asarray(a, dtype=np.float32))
    x_prompt = np.asarray(inputs['x_prompt'], dtype=np.float32)
    x_sample = f32(np.asarray(inputs['x_sample'])[0])
    c_prompt = np.asarray(inputs['c_prompt'], dtype=np.float32)
    c_sample = np.asarray(inputs['c_sample'], dtype=np.float32)
    w_ada = f32(np.asarray(inputs['w_ada'])[0])
    b_ada = np.asarray(inputs['b_ada'], dtype=np.float32)[0]
    shared = {
        'xsf': x_sample,
        'w_ada': w_ada,
        'b_ada_col': f32(b_ada.reshape(96, 128).T),
        'b_ada_row': f32(b_ada.reshape(1, 12288)),
        'w_in': f32(np.asarray(inputs['w_in'])[0]),
        'w_four': f32(np.asarray(inputs['w_fourier'])[0]),
        'w_out': f32(np.asarray(inputs['w_out'])[0]),
        'w_mi': f32(np.asarray(inputs['w_mlp_in'])[0]),
        'w_mo': f32(np.asarray(inputs['w_mlp_out'])[0]),
        'tabc': C['tabc'], 'tabp': C['tabp'], 'dmask': C['dmask'],
        'identf': C['identf'], 'identb': C['identb'],
    }
    gout = np.concatenate([np.asarray(inputs['g_attn_out'], dtype=np.float32)[0],
                           np.asarray(inputs['g_fourier_out'], dtype=np.float32)[0]])
    gc = np.stack([np.asarray(inputs['g_pre_mix'], dtype=np.float32)[0].reshape(16, 128).T,
                   np.asarray(inputs['g_pre_mlp'], dtype=np.float32)[0].reshape(16, 128).T,
                   gout.reshape(16, 128).T], axis=1)
    shared['gcols'] = f32(gc.reshape(128, 48))
    shared['grows'] = f32(np.stack([np.asarray(inputs['g_post_mix'], dtype=np.float32)[0],
                                    np.asarray(inputs['g_post_mlp'], dtype=np.float32)[0]]))
    maps = []
    for core in range(n_cores):
        m = dict(shared)
        m['xp'] = np.ascontiguousarray(x_prompt[2 * core:2 * core + 2])
        xw = np.zeros((3072, 2048), dtype=np.float32)
        lo, hi = 1024 * (core - 1), 1024 * (core + 2)
        a, b = max(lo, 0), min(hi, 8192)
        xw[a - lo:b - lo] = x_sample[a:b]
        m['xsw'] = xw
        cs = np.stack([c_prompt[2 * core], c_prompt[2 * core + 1], c_sample[0]], axis=1)
        m['ccol'] = f32(cs.reshape(16, 128, 3).transpose(1, 0, 2).reshape(128, 48))
        m['tabs'] = C['tabs'][core]
        m['edge'] = C['edge'][core]
        maps.append(m)
    return maps


_NC = {}


def get_nc(cfg=None):
    key = repr(cfg)
    if key not in _NC:
        nc, P, stack = build(cfg)
        P.emit(nc, stack)
        stack.close()
        _NC[key] = nc
    return _NC[key]


def kernel(**inputs):
    maps = make_in_maps(inputs)
    nc = get_nc(None)
    res = run_bass_kernel_spmd(nc, maps, core_ids=list(range(8)))
    y_prompt = np.empty((16, 2048, 2048), dtype=np.float32)
    y_sample = np.empty((1, 8192, 2048), dtype=np.float32)
    for core in range(8):
        r = res.results[core]
        y_prompt[2 * core:2 * core + 2] = np.asarray(r['yp'], dtype=np.float32)
        y_sample[0, 1024 * core:1024 * (core + 1)] = np.asarray(r['ys'], dtype=np.float32)
    return (y_prompt, y_sample)
```
